# Optimizing a Trainium2 kernel written in Bass

```python
import jax
import jax.numpy as jnp
from jax import lax
import numpy as np

D_MODEL = 2048
BATCH = 4
SEQ = 2048
DEPTH = 4
DEC_BATCH = 32
DEC_SEQ = 1
PAST_LEN = 16384
PAGE_SIZE = 128

N_MIXERS = 3
LAYER_MIXER = tuple(i % N_MIXERS for i in range(DEPTH))
LAYER_SLOT = tuple(LAYER_MIXER[:i].count(LAYER_MIXER[i]) for i in range(DEPTH))
N_A = LAYER_MIXER.count(0)
N_B = LAYER_MIXER.count(1)
N_C = LAYER_MIXER.count(2)
EPS = 1e-6

D_RNN = 2560
LRU_BLOCKS = 16
LRU_BLOCK = D_RNN // LRU_BLOCKS
CONV_A = 4
LRU_C = 8.0

N_HEADS = 32
N_KV = 4
HEAD_DIM = 64
GROUP = N_HEADS // N_KV
WINDOW = 128
Q_BLOCK = 128
ROT_DIM = HEAD_DIM // 4
ROPE_THETA = 500000.0

HG_HEADS = 16
HG_DK = 128
HG_DV = D_MODEL // HG_HEADS
HG_CHUNK = 64

D_FF = 3 * D_MODEL
CONV_F = 3

kernel_name = "hybrid_rglru_swa_hgrn2_convffn_step"

F32 = jnp.float32


def rms_norm(x, g):
    xf = x.astype(F32)
    y = xf * lax.rsqrt(jnp.mean(xf * xf, axis=-1, keepdims=True) + EPS)
    return (y * g.astype(F32)).astype(x.dtype)


def causal_dwconv(x, buf, w, b):
    width, T = w.shape[0], x.shape[1]
    ext = jnp.concatenate([buf.astype(x.dtype), x], axis=1)
    y = b + ext[:, width - 1:width - 1 + T] * w[width - 1]
    for j in range(width - 1):
        y = y + ext[:, j:j + T] * w[j]
    return y, ext[:, T:]


def rglru(x, h0, w_a, b_a, w_i, b_i, lam):
    B, T, _ = x.shape
    xb = x.reshape(B, T, LRU_BLOCKS, LRU_BLOCK)
    r = jax.nn.sigmoid(jnp.einsum('btnc,ncd->btnd', xb, w_a).reshape(B, T, D_RNN).astype(F32) + b_a.astype(F32))
    i = jax.nn.sigmoid(jnp.einsum('btnc,ncd->btnd', xb, w_i).reshape(B, T, D_RNN).astype(F32) + b_i.astype(F32))
    log_a = -LRU_C * r * jax.nn.softplus(-lam.astype(F32))
    a = jnp.exp(log_a)
    u = jnp.sqrt(-jnp.expm1(2.0 * log_a)) * (i * x.astype(F32))
    u = u.at[:, 0].add(a[:, 0] * h0.astype(F32))

    def combine(left, right):
        a1, b1 = left
        a2, b2 = right
        return a1 * a2, a2 * b1 + b2

    _, h = lax.associative_scan(combine, (a, u), axis=1)
    return h.astype(x.dtype), h[:, -1].astype(x.dtype)


def mixer_lru(x, h0, conv_buf, w_in, conv_w, conv_b, w_a, b_a, w_i, b_i, lam, w_out):
    gate, xr = jnp.split(x @ w_in, 2, axis=-1)
    xr, new_buf = causal_dwconv(xr, conv_buf, conv_w, conv_b)
    h, h_last = rglru(xr, h0, w_a, b_a, w_i, b_i, lam)
    y = (jax.nn.gelu(gate) * h) @ w_out
    return y, h_last, new_buf


def rope_partial(x, pos):
    half = ROT_DIM // 2
    inv = ROPE_THETA ** (-jnp.arange(half, dtype=F32) * (2.0 / ROT_DIM))
    ang = pos[:, None] * inv[None, :]
    cos = jnp.cos(ang)[None, :, None, :]
    sin = jnp.sin(ang)[None, :, None, :]
    xr = x[..., :ROT_DIM].astype(F32)
    x1, x2 = xr[..., :half], xr[..., half:]
    rot = jnp.concatenate([x1 * cos - x2 * sin, x2 * cos + x1 * sin], axis=-1)
    return jnp.concatenate([rot.astype(x.dtype), x[..., ROT_DIM:]], axis=-1)


def mixer_swa(x, pos0, k_buf, v_buf, w_qkv, b_qkv, sinks, w_o, b_o):
    B, T, _ = x.shape
    qkv = x @ w_qkv + b_qkv
    q, k, v = jnp.split(qkv, [N_HEADS * HEAD_DIM, (N_HEADS + N_KV) * HEAD_DIM], axis=-1)
    pos = (pos0 + jnp.arange(T)).astype(F32)
    q = rope_partial(q.reshape(B, T, N_HEADS, HEAD_DIM), pos)
    k = rope_partial(k.reshape(B, T, N_KV, HEAD_DIM), pos)
    v = v.reshape(B, T, N_KV, HEAD_DIM)
    W = k_buf.shape[1]
    k_ext = jnp.concatenate([k_buf.astype(x.dtype), k], axis=1)
    v_ext = jnp.concatenate([v_buf.astype(x.dtype), v], axis=1)
    bq = Q_BLOCK if T % Q_BLOCK == 0 else T
    nb = T // bq
    nk = W + bq
    kidx = jnp.arange(nb)[:, None] * bq + jnp.arange(nk)[None, :]
    kb = k_ext[:, kidx]
    vb = v_ext[:, kidx]
    qb = q.reshape(B, nb, bq, N_KV, GROUP, HEAD_DIM)
    q_pos = pos0 + jnp.arange(T).reshape(nb, bq)
    k_pos = pos0 - W + kidx
    rel = q_pos[:, :, None] - k_pos[:, None, :]
    valid = (rel >= 0) & (rel < WINDOW) & (k_pos[:, None, :] >= 0)
    s = jnp.einsum('bnqkgd,bnskd->bnkgqs', qb, kb, preferred_element_type=F32) * (HEAD_DIM ** -0.5)
    s = jnp.where(valid[None, :, None, None], s, -jnp.inf)
    sink = sinks.astype(F32).reshape(N_KV, GROUP)[None, None, :, :, None, None]
    m = jnp.maximum(jnp.max(s, axis=-1, keepdims=True), sink)
    p = jnp.exp(s - m)
    p = p / (jnp.sum(p, axis=-1, keepdims=True) + jnp.exp(sink - m))
    o = jnp.einsum('bnkgqs,bnskd->bnqkgd', p.astype(x.dtype), vb).reshape(B, T, N_HEADS * HEAD_DIM)
    y = o @ w_o + b_o
    return y, k_ext[:, -W:], v_ext[:, -W:]


def hgrn_chunked(q, k, log_f, v, S0):
    B, T, H, _ = q.shape
    C = HG_CHUNK if T >= HG_CHUNK else T
    Tp = -(-T // C) * C
    pad = Tp - T
    if pad:
        padw = ((0, 0), (0, pad), (0, 0), (0, 0))
        q, k, log_f, v = (jnp.pad(a, padw) for a in (q, k, log_f, v))
    nc = Tp // C

    def to_chunks(a):
        return a.reshape(B, nc, C, H, a.shape[-1]).transpose(1, 0, 3, 2, 4)

    mask = jnp.tril(jnp.ones((C, C), dtype=bool))[:, :, None]

    def step(S, inp):
        qc, kc, lfc, vc = inp
        b = jnp.cumsum(lfc, axis=2)
        diff = b[:, :, :, None, :] - b[:, :, None, :, :]
        decay = jnp.exp(jnp.where(mask, diff, -jnp.inf))
        A = jnp.einsum('bhtd,bhsd,bhtsd->bhts', qc, kc, decay)
        o = jnp.einsum('bhts,bhse->bhte', A, vc) + jnp.einsum('bhtd,bhde->bhte', qc * jnp.exp(b), S)
        b_last = b[:, :, -1, :]
        S = S * jnp.exp(b_last)[..., None] + jnp.einsum('bhsd,bhse->bhde', kc * jnp.exp(b_last[:, :, None, :] - b), vc)
        return S, o

    S, o = lax.scan(step, S0, (to_chunks(q), to_chunks(k), to_chunks(log_f), to_chunks(v)))
    o = o.transpose(1, 0, 3, 2, 4).reshape(B, Tp, H, v.shape[-1])[:, :T]
    return o, S


def mixer_hgrn2(x, S0, lb, w_in, g_norm, w_o):
    B, T, _ = x.shape
    fk = HG_HEADS * HG_DK
    fv = HG_HEADS * HG_DV
    q, z, v, g = jnp.split(x @ w_in, [fk, 2 * fk, 2 * fk + fv], axis=-1)
    zf = z.astype(F32)
    log_f = jnp.logaddexp(jnp.log(lb), jnp.log1p(-lb) + jax.nn.log_sigmoid(zf))
    k = (1.0 - lb) * jax.nn.sigmoid(-zf)
    shp_k = (B, T, HG_HEADS, HG_DK)
    shp_v = (B, T, HG_HEADS, HG_DV)
    o, S = hgrn_chunked(jax.nn.silu(q.astype(F32)).reshape(shp_k), k.reshape(shp_k), log_f.reshape(shp_k),
                        v.astype(F32).reshape(shp_v), S0.astype(F32))
    o = rms_norm(o, g_norm) * jax.nn.silu(g.astype(F32)).reshape(shp_v)
    y = o.reshape(B, T, fv).astype(x.dtype) @ w_o
    return y, S.astype(x.dtype)


def conv_ffn(x, buf, w_in, conv_w, conv_b, w_out):
    gate, up = jnp.split(x @ w_in, 2, axis=-1)
    gate, new_buf = causal_dwconv(gate, buf, conv_w, conv_b)
    return (jax.nn.gelu(gate) * up) @ w_out, new_buf


def run_trunk(x, pos0, lru_h, lru_conv, swa_k, swa_v, hg_S, ffn_conv, weights):
    (norm_mix, norm_ffn, norm_final,
     lru_w_in, lru_conv_w, lru_conv_b, lru_w_a, lru_b_a, lru_w_i, lru_b_i, lru_lambda, lru_w_out,
     swa_w_qkv, swa_b_qkv, swa_sinks, swa_w_o, swa_b_o,
     hg_w_in, hg_lb_logits, hg_norm, hg_w_o,
     ffn_w_in, ffn_conv_w, ffn_conv_b, ffn_w_out) = weights
    sm = jax.nn.softmax(hg_lb_logits.astype(F32), axis=0)
    lb_all = jnp.cumsum(sm, axis=0) - sm[0]
    n_h, n_c, n_k, n_v, n_s, n_f = [], [], [], [], [], []
    for layer in range(DEPTH):
        kind, j = LAYER_MIXER[layer], LAYER_SLOT[layer]
        hn = rms_norm(x, norm_mix[layer])
        if kind == 0:
            y, h_new, c_new = mixer_lru(hn, lru_h[j], lru_conv[j], lru_w_in[j], lru_conv_w[j], lru_conv_b[j],
                                        lru_w_a[j], lru_b_a[j], lru_w_i[j], lru_b_i[j], lru_lambda[j], lru_w_out[j])
            n_h.append(h_new)
            n_c.append(c_new)
        elif kind == 1:
            y, k_new, v_new = mixer_swa(hn, pos0, swa_k[j], swa_v[j], swa_w_qkv[j], swa_b_qkv[j], swa_sinks[j],
                                        swa_w_o[j], swa_b_o[j])
            n_k.append(k_new)
            n_v.append(v_new)
        else:
            y, s_new = mixer_hgrn2(hn, hg_S[j], lb_all[layer], hg_w_in[j], hg_norm[j], hg_w_o[j])
            n_s.append(s_new)
        x = x + y
        y, f_new = conv_ffn(rms_norm(x, norm_ffn[layer]), ffn_conv[layer], ffn_w_in[layer], ffn_conv_w[layer],
                            ffn_conv_b[layer], ffn_w_out[layer])
        n_f.append(f_new)
        x = x + y
    return (rms_norm(x, norm_final), jnp.stack(n_h), jnp.stack(n_c), jnp.stack(n_k), jnp.stack(n_v),
            jnp.stack(n_s), jnp.stack(n_f))


def setup_inputs(seed: int = 0) -> dict:
    key = jax.random.key(seed)
    keys = iter(jax.random.split(key, 48))

    def nrm(shape, scale):
        return jax.random.normal(next(keys), shape, F32) * scale

    wb = min(WINDOW, PAST_LEN)
    u = jax.random.uniform(next(keys), (N_A, D_RNN), F32, minval=0.9, maxval=0.999)
    return {
        "x_prompt": nrm((BATCH, SEQ, D_MODEL), 1.0),
        "x_sample": nrm((DEC_BATCH, DEC_SEQ, D_MODEL), 1.0),
        "state_lru_h": nrm((N_A, DEC_BATCH, D_RNN), 1.0),
        "state_lru_conv": nrm((N_A, DEC_BATCH, CONV_A - 1, D_RNN), 1.0),
        "cache_swa_k": nrm((N_B, DEC_BATCH, wb, N_KV, HEAD_DIM), 1.0),
        "cache_swa_v": nrm((N_B, DEC_BATCH, wb, N_KV, HEAD_DIM), 1.0),
        "state_hgrn": nrm((N_C, DEC_BATCH, HG_HEADS, HG_DK, HG_DV), 1.0),
        "state_ffn_conv": nrm((DEPTH, DEC_BATCH, CONV_F - 1, D_FF), 1.0),
        "norm_mix": 1.0 + nrm((DEPTH, D_MODEL), 0.02),
        "norm_ffn": 1.0 + nrm((DEPTH, D_MODEL), 0.02),
        "norm_final": 1.0 + nrm((D_MODEL,), 0.02),
        "lru_w_in": nrm((N_A, D_MODEL, 2 * D_RNN), D_MODEL ** -0.5),
        "lru_conv_w": nrm((N_A, CONV_A, D_RNN), CONV_A ** -0.5),
        "lru_conv_b": nrm((N_A, D_RNN), 0.01),
        "lru_w_a": nrm((N_A, LRU_BLOCKS, LRU_BLOCK, LRU_BLOCK), LRU_BLOCK ** -0.5),
        "lru_b_a": nrm((N_A, D_RNN), 0.01),
        "lru_w_i": nrm((N_A, LRU_BLOCKS, LRU_BLOCK, LRU_BLOCK), LRU_BLOCK ** -0.5),
        "lru_b_i": nrm((N_A, D_RNN), 0.01),
        "lru_lambda": jnp.log(u) - jnp.log1p(-u),
        "lru_w_out": nrm((N_A, D_RNN, D_MODEL), D_RNN ** -0.5),
        "swa_w_qkv": nrm((N_B, D_MODEL, (N_HEADS + 2 * N_KV) * HEAD_DIM), D_MODEL ** -0.5),
        "swa_b_qkv": nrm((N_B, (N_HEADS + 2 * N_KV) * HEAD_DIM), 0.01),
        "swa_sinks": nrm((N_B, N_HEADS), 1.0),
        "swa_w_o": nrm((N_B, N_HEADS * HEAD_DIM, D_MODEL), (N_HEADS * HEAD_DIM) ** -0.5),
        "swa_b_o": nrm((N_B, D_MODEL), 0.01),
        "hg_w_in": nrm((N_C, D_MODEL, 2 * HG_HEADS * HG_DK + 2 * HG_HEADS * HG_DV), D_MODEL ** -0.5),
        "hg_lb_logits": nrm((DEPTH, HG_HEADS * HG_DK), 0.5),
        "hg_norm": 1.0 + nrm((N_C, HG_DV), 0.02),
        "hg_w_o": nrm((N_C, HG_HEADS * HG_DV, D_MODEL), (HG_HEADS * HG_DV) ** -0.5),
        "ffn_w_in": nrm((DEPTH, D_MODEL, 2 * D_FF), D_MODEL ** -0.5),
        "ffn_conv_w": nrm((DEPTH, CONV_F, D_FF), CONV_F ** -0.5),
        "ffn_conv_b": nrm((DEPTH, D_FF), 0.01),
        "ffn_w_out": nrm((DEPTH, D_FF, D_MODEL), D_FF ** -0.5),
    }


def reference(x_prompt, x_sample, state_lru_h, state_lru_conv, cache_swa_k, cache_swa_v, state_hgrn, state_ffn_conv,
              norm_mix, norm_ffn, norm_final,
              lru_w_in, lru_conv_w, lru_conv_b, lru_w_a, lru_b_a, lru_w_i, lru_b_i, lru_lambda, lru_w_out,
              swa_w_qkv, swa_b_qkv, swa_sinks, swa_w_o, swa_b_o,
              hg_w_in, hg_lb_logits, hg_norm, hg_w_o,
              ffn_w_in, ffn_conv_w, ffn_conv_b, ffn_w_out):
    weights = (norm_mix, norm_ffn, norm_final,
               lru_w_in, lru_conv_w, lru_conv_b, lru_w_a, lru_b_a, lru_w_i, lru_b_i, lru_lambda, lru_w_out,
               swa_w_qkv, swa_b_qkv, swa_sinks, swa_w_o, swa_b_o,
               hg_w_in, hg_lb_logits, hg_norm, hg_w_o,
               ffn_w_in, ffn_conv_w, ffn_conv_b, ffn_w_out)
    nb = x_prompt.shape[0]

    def empty(a):
        return jnp.zeros((a.shape[0], nb) + a.shape[2:], x_prompt.dtype)

    y_prompt, p_h, p_c, p_k, p_v, p_s, p_f = run_trunk(
        x_prompt, 0, empty(state_lru_h), empty(state_lru_conv), empty(cache_swa_k), empty(cache_swa_v),
        empty(state_hgrn), empty(state_ffn_conv), weights)
    y_sample, s_h, s_c, s_k, s_v, s_s, s_f = run_trunk(
        x_sample, PAST_LEN, state_lru_h, state_lru_conv, cache_swa_k, cache_swa_v, state_hgrn, state_ffn_conv, weights)
    return (y_prompt, y_sample, p_h, p_c, p_k, p_v, p_s, p_f, s_h, s_c, s_k, s_v, s_s, s_f)
```

```python
import numpy as np
from contextlib import ExitStack
import concourse.bass as bass
import concourse.mybir as mybir
from concourse.bass_utils import run_bass_kernel_spmd

F32 = mybir.dt.float32
BF16 = mybir.dt.bfloat16
AF = mybir.ActivationFunctionType
ALU = mybir.AluOpType

D = 2048
NDC = 16
DR = 2560
NRC = 20
DFF = 6144
NFC = 48
TT = 512
SEQ = 2048
NT = SEQ // TT
NS = 4
EPS = 1e-6
DEPTH = 4
PAST = 16384
MIX = (0, 1, 2, 0)
SLOT = (0, 0, 0, 1)
GELU_C = 1.5957691216057308


class Reg:
    __slots__ = ("name", "lw", "rd")

    def __init__(self, name):
        self.name = name
        self.lw = None
        self.rd = []


class Op:
    __slots__ = ("eng", "fn", "deps", "is_dma", "marked", "idx", "dsem", "dtarget")

    def __init__(self, eng, fn, is_dma):
        self.eng = eng
        self.fn = fn
        self.deps = []
        self.is_dma = is_dma
        self.marked = False
        self.idx = 0
        self.dsem = None
        self.dtarget = 0


ENGS = ("pe", "act", "dve", "pool", "sp")


class Sched:
    N_DMA_SEMS = 32

    def __init__(self, nc, same_engine_sync=True):
        self.nc = nc
        self.ops = []
        self.same_engine_sync = same_engine_sync
        self.dma_rr = 0
        self.dma_count = [0] * self.N_DMA_SEMS
        self.dma_last = [None] * self.N_DMA_SEMS

    def _add(self, op, r, w):
        deps = set()
        for reg in r:
            if reg.lw is not None:
                deps.add(reg.lw)
        for reg in w:
            if reg.lw is not None:
                deps.add(reg.lw)
            for o in reg.rd:
                deps.add(o)
        deps.discard(op)
        op.deps = list(deps)
        for reg in r:
            if not op.is_dma:
                reg.rd = [o for o in reg.rd if o.is_dma or o.eng != op.eng]
            reg.rd.append(op)
        for reg in w:
            reg.lw = op
            reg.rd = []
        self.ops.append(op)
        return op

    def op(self, eng, fn, r=(), w=()):
        return self._add(Op(eng, fn, False), r, w)

    def dma(self, eng, fn, r=(), w=()):
        op = Op(eng, fn, True)
        k = self.dma_rr
        self.dma_rr = (self.dma_rr + 1) % self.N_DMA_SEMS
        op.dsem = k
        self.dma_count[k] += 1
        op.dtarget = 16 * self.dma_count[k]
        prev = self.dma_last[k]
        self.dma_last[k] = op
        self._add(op, r, w)
        if prev is not None:
            op.deps.append(prev)
        return op

    def emit(self, stack):
        nc = self.nc
        sync_same = self.same_engine_sync
        for o in self.ops:
            for d in o.deps:
                if d.is_dma:
                    continue
                if d.eng == o.eng and not o.is_dma and (d.eng == "pe" or not sync_same):
                    continue
                d.marked = True
        cnt = {e: 0 for e in ENGS}
        for o in self.ops:
            if o.marked and not o.is_dma:
                cnt[o.eng] += 1
                o.idx = cnt[o.eng]
        esem = {e: stack.enter_context(nc.semaphore("es_" + e)) for e in ENGS}
        dsem = [stack.enter_context(nc.semaphore("ds_%d" % i)) for i in range(self.N_DMA_SEMS)]
        per = {e: [] for e in ENGS}
        for o in self.ops:
            per[o.eng].append(o)
        final_dma = [(dsem[k], 16 * self.dma_count[k]) for k in range(self.N_DMA_SEMS) if self.dma_count[k]]
        final_eng = [(esem[e], cnt[e]) for e in ENGS if cnt[e]]

        def run(ename, eng):
            waited = {}
            for o in per[ename]:
                need = {}
                for d in o.deps:
                    if d.is_dma:
                        key = ("d", d.dsem)
                        val = d.dtarget
                        sem = dsem[d.dsem]
                    else:
                        if d.eng == ename and not o.is_dma and (ename == "pe" or not sync_same):
                            continue
                        key = ("e", d.eng)
                        val = d.idx
                        sem = esem[d.eng]
                    if val > need.get(key, (None, 0))[1]:
                        need[key] = (sem, val)
                for key, (sem, val) in need.items():
                    if waited.get(key, 0) >= val:
                        continue
                    eng.wait_ge(sem, val)
                    waited[key] = val
                ins = o.fn(eng)
                if o.is_dma:
                    ins.then_inc(dsem[o.dsem], 16)
                elif o.marked:
                    ins.then_inc(esem[ename], 1)
            if ename == "sp":
                for sem, val in final_dma + final_eng:
                    eng.wait_ge(sem, val)

        block = stack.enter_context(nc.Block())

        @block.sync
        def _(e):
            run("sp", e)

        @block.tensor
        def _(e):
            run("pe", e)

        @block.scalar
        def _(e):
            run("act", e)

        @block.vector
        def _(e):
            run("dve", e)

        @block.gpsimd
        def _(e):
            run("pool", e)


def fm(v):
    v = np.asarray(v, np.float32)
    sh = v.shape
    v = v.reshape(sh[:-1] + (sh[-1] // 128, 128))
    return np.ascontiguousarray(np.moveaxis(v, -1, 0))


def band_blocks():
    out = []
    for j in range(NRC):
        b0 = (128 * j) // 160
        b1 = (128 * j + 127) // 160
        k0 = (160 * b0) // 128
        k1 = (160 * (b1 + 1) - 1) // 128
        out.append(list(range(k0, k1 + 1)))
    return out


BAND = band_blocks()
NBAND = sum(len(b) for b in BAND)


def band_layout(w):
    dense = np.zeros((DR, DR), np.float32)
    for n in range(16):
        dense[160 * n:160 * n + 160, 160 * n:160 * n + 160] = w[n]
    res = np.zeros((128, 64 * 128), np.float32)
    bi = 0
    for j in range(NRC):
        for kc in BAND[j]:
            res[:, bi * 128:(bi + 1) * 128] = dense[kc * 128:(kc + 1) * 128, j * 128:(j + 1) * 128]
            bi += 1
    return res


class PP:
    def __init__(self):
        self.cols = []
        self.off = {}
        self.n = 0

    def add(self, name, arr):
        arr = np.asarray(arr, np.float32).reshape(128, -1)
        self.off[name] = (self.n, arr.shape[1])
        self.cols.append(arr)
        self.n += arr.shape[1]

    def array(self):
        return np.ascontiguousarray(np.concatenate(self.cols, axis=1))


def rope_tables(pos):
    pos = np.asarray(pos, np.float32)
    half = 8
    inv = (np.float32(500000.0) ** (-np.arange(half, dtype=np.float32) * np.float32(2.0 / 16))).astype(np.float32)
    ang = pos[None, :] * inv[:, None]
    cos = np.cos(ang).astype(np.float32)
    sin = np.sin(ang).astype(np.float32)
    C = np.ones((128, len(pos)), np.float32)
    Sg = np.zeros((128, len(pos)), np.float32)
    for p in range(128):
        d = p % 64
        if d < 8:
            C[p] = cos[d]
            Sg[p] = -sin[d]
        elif d < 16:
            C[p] = cos[d - 8]
            Sg[p] = sin[d - 8]
    return C, Sg


def qk_cols():
    cols = list(range(2048))
    for g in range(4):
        base = 2048 + 64 * g
        cols += list(range(base, base + 64)) * 2
    cols = np.array(cols)
    rot = cols.copy()
    for i, c in enumerate(cols):
        hd = c % 64
        if hd < 8:
            rot[i] = c + 8
        elif hd < 16:
            rot[i] = c - 8
    return cols, rot


def v_cols():
    cols = []
    for g in range(4):
        base = 2304 + 64 * g
        cols += list(range(base, base + 64)) * 2
    return np.array(cols)


class Buf:
    def __init__(self, B, off, nch, C, dt, name):
        self.dt = dt
        self.nch = nch
        self.C = C
        cw = C if dt == F32 else C // 2
        self.cw = cw
        self.off = off
        self.words = nch * cw
        assert off + self.words <= B.ARENA, (name, off, self.words)
        raw = B.arena[:, off:off + nch * cw]
        if dt != F32:
            raw = raw.bitcast(dt)
        self.ap = raw.rearrange("p (n c) -> p n c", n=nch)
        self.B = B

    def r(self, c0=0, c1=None):
        if c1 is None:
            c1 = c0 + 1
        return self.B.aregs(self.off + c0 * self.cw, self.off + c1 * self.cw)

    def all(self):
        return self.r(0, self.nch)

    @property
    def end(self):
        return self.off + self.words


GELU_F = AF.Gelu_apprx_tanh
SAME_ENGINE_SYNC = True
NCST = 128 + 128 + 512 + 512
NMSK = 512 + 512 + 128


class Builder:
    ARENA = 19456
    ABLK = 256
    NORM_OFF = 18432 - 1536

    def __init__(self):
        nc = self.nc = bass.Bass("TRN2", target_bir_lowering=False)
        self.st = ExitStack()
        self.S = Sched(nc, same_engine_sync=SAME_ENGINE_SYNC)
        self.din = {}
        self.dout = {}
        self._bank_i = 0
        self._slab_i = 0
        self._reserved = set()
        self.slab_cache = {}
        self._slab_pinned = set()
        self._slab_n = 0
        self._deferred = set()
        self._defer_frac = 0
        self._last_slab = 0
        self.use_cache = True

    def inp(self, name, shape):
        t = self.nc.dram_tensor(name, list(shape), F32, kind="ExternalInput").ap()
        self.din[name] = tuple(shape)
        return t

    def outp(self, name, shape):
        t = self.nc.dram_tensor(name, list(shape), F32, kind="ExternalOutput").ap()
        self.dout[name] = tuple(shape)
        return t

    def sb(self, name, shape, dt=F32):
        return self.st.enter_context(self.nc.sbuf_tensor(name, list(shape), dt))

    def aregs(self, w0, w1):
        return self.areg[w0 // self.ABLK:(w1 + self.ABLK - 1) // self.ABLK]

    def bank(self):
        while self._bank_i in self._reserved:
            self._bank_i = (self._bank_i + 1) % 8
        i = self._bank_i
        self._bank_i = (i + 1) % 8
        return self.ps[i], self.rps[i]

    def op(self, eng, fn, r=(), w=()):
        return self.S.op(eng, fn, r, w)

    def dma(self, fn, r=(), w=(), eng="sp"):
        return self.S.dma(eng, fn, r, w)

    def next_slab(self):
        while self._slab_i in self._slab_pinned:
            self._slab_i = (self._slab_i + 1) % 3
        i = self._slab_i
        self._slab_i = (i + 1) % 3
        self._last_slab = i
        return self.slabs[i], self.rslab[i]

    def load_slab(self, W2d, k0, nk, c0, ncol):
        t, regs = self.next_slab()
        key = (W2d.tensor.name, k0, nk, c0, ncol)
        hit = self.slab_cache.get(key)
        if hit is not None:
            scr = hit
            for a in range(0, nk, 8):
                b = min(nk, a + 8)
                h = a // 8
                self.S.dma("sp", lambda e, t=t, scr=scr, a=a, b=b, ncol=ncol: e.dma_start(out=t[:, a:b, 0:ncol], in_=scr[:, a:b, 0:ncol]),
                           w=[regs[2 * h], regs[2 * h + 1]])
            return t, (lambda kc: [regs[kc // 4]])
        src = W2d[k0 * 128:(k0 + nk) * 128, c0:c0 + ncol].rearrange("(kc p) n -> p kc n", p=128)
        for a in range(0, nk, 8):
            b = min(nk, a + 8)
            h = a // 8
            self.S.dma("pool", lambda e, t=t, src=src, a=a, b=b, ncol=ncol: e.dma_start(out=t[:, a:b, 0:ncol], in_=src[:, a:b, :]),
                       w=[regs[2 * h], regs[2 * h + 1]])
        self._slab_n += 1
        if self.use_cache and not (self._defer_frac and key not in self._deferred and (self._slab_n % 10) < self._defer_frac
                                   and not self._deferred.add(key)):
            scr = self.nc.dram_tensor("wscr%d" % len(self.slab_cache), [128, nk, ncol], BF16).ap()
            self.slab_cache[key] = scr
            for a in range(0, nk, 8):
                b = min(nk, a + 8)
                h = a // 8
                self.S.dma("sp", lambda e, t=t, scr=scr, a=a, b=b, ncol=ncol: e.dma_start(out=scr[:, a:b, 0:ncol], in_=t[:, a:b, 0:ncol]),
                           r=[regs[2 * h], regs[2 * h + 1]])
        return t, (lambda kc: [regs[kc // 4]])

    def load_band(self, Wb):
        t, regs = self.next_slab()
        flat = t[:].rearrange("p a b -> p (a b)")
        key = (Wb.tensor.name, "band")
        hit = self.slab_cache.get(key)
        if hit is not None:
            for q in range(4):
                self.S.dma("sp", lambda e, q=q, hit=hit: e.dma_start(out=flat[:, q * 2048:(q + 1) * 2048], in_=hit[:, q * 2048:(q + 1) * 2048]),
                           w=[regs[q]])
            return flat, (lambda b: [regs[b // 16]])
        for q in range(4):
            self.S.dma("pool", lambda e, q=q: e.dma_start(out=flat[:, q * 2048:(q + 1) * 2048], in_=Wb[:, q * 2048:(q + 1) * 2048]),
                       w=[regs[q]])
        if self.use_cache:
            scr = self.nc.dram_tensor("wscr%d" % len(self.slab_cache), [128, 8192], BF16).ap()
            self.slab_cache[key] = scr
            for q in range(4):
                self.S.dma("sp", lambda e, q=q, scr=scr: e.dma_start(out=scr[:, q * 2048:(q + 1) * 2048], in_=flat[:, q * 2048:(q + 1) * 2048]),
                           r=[regs[q]])
        return flat, (lambda b: [regs[b // 16]])

    def proj_fm(self, W2d, nkc, inp, rin, C, c_start, nout, evac, kslab=16):
        for g0 in range(0, nout, 4):
            ng = min(4, nout - g0)
            banks = [self.bank() for _ in range(ng)]
            nks = (nkc + kslab - 1) // kslab
            for ks in range(nks):
                k0 = ks * kslab
                nk = min(kslab, nkc - k0)
                slab, rs = self.load_slab(W2d, k0, nk, c_start + g0 * 128, ng * 128)
                for jj in range(ng):
                    pst, prg = banks[jj]
                    for kc in range(nk):
                        self.op("pe", lambda e, pst=pst, slab=slab, kc=kc, jj=jj, k0=k0, first=(ks == 0 and kc == 0),
                                last=(ks == nks - 1 and kc == nk - 1): e.matmul(
                            pst[:, 0:C], lhsT=slab[:, kc, jj * 128:(jj + 1) * 128], rhs=inp[:, k0 + kc, 0:C], start=first, stop=last),
                            r=rs(kc) + rin(k0 + kc), w=[prg])
            idx = [self.ps.index(bk[0]) for bk in banks]
            self._reserved.update(idx)
            for jj in range(ng):
                self._reserved.discard(idx[jj])
                evac(g0 + jj, banks[jj][0], banks[jj][1])

    def setup(self, pp_off, npp):
        nc = self.nc
        self.pp_off = pp_off
        self.x = self.sb("x", [128, NDC, TT])
        self.rx = [Reg("x%d" % c) for c in range(NDC)]
        self.xn = self.sb("xn", [128, NDC, TT], BF16)
        self.rxn = [Reg("xn%d" % c) for c in range(NDC)]
        self.slabs = [self.sb("slab%d" % i, [128, 16, 512], BF16) for i in range(3)]
        self.rslab = [[Reg("slab%d_%d" % (i, q)) for q in range(4)] for i in range(3)]
        self.ps = [self.st.enter_context(nc.psum_tensor("ps%d" % i, [128, 512], F32)) for i in range(8)]
        self.rps = [Reg("ps%d" % i) for i in range(8)]
        self.arena = self.sb("arena", [128, self.ARENA])
        self.areg = [Reg("ar%d" % i) for i in range(self.ARENA // self.ABLK)]
        self.pp = self.sb("pp_sb", [128, npp])
        self.rpp = Reg("pp")
        self.ppd = self.inp("pp", [128, npp])
        self.dma(lambda e: e.dma_start(out=self.pp[:], in_=self.ppd), w=[self.rpp])
        self.cst_d = self.inp("cst", [128, NCST])
        self.cst = self.sb("cstt", [128, NCST])
        self.rcst = Reg("cst")
        self.dma(lambda e: e.dma_start(out=self.cst[:], in_=self.cst_d), w=[self.rcst])
        self.ident = self.cst[:, 0:128]
        self.ones = self.cst[:, 128:256]
        self.m01 = self.cst[:, 256:768]
        self.sel = self.cst[:, 768:1280]
        self.msk_d = self.inp("msk", [128, NMSK])
        mtmp = self.arena[:, 0:NMSK]
        mr = self.aregs(0, NMSK)
        self.dma(lambda e: e.dma_start(out=mtmp, in_=self.msk_d), w=mr)
        self.cstb = self.sb("cstb", [128, NMSK + 128], BF16)
        self.rcstb = Reg("cstb")
        self.op("dve", lambda e: e.tensor_copy(out=self.cstb[:, 0:NMSK], in_=mtmp), r=mr, w=[self.rcstb])
        self.op("dve", lambda e: e.tensor_copy(out=self.cstb[:, NMSK:NMSK + 128], in_=self.cst[:, 128:256]), r=[self.rcst], w=[self.rcstb])
        self.mprev4 = self.cstb[:, 0:512]
        self.mcur4 = self.cstb[:, 512:1024]
        self.mhg = self.cstb[:, 1024:1152]
        self.onesb = self.cstb[:, 1152:1280]
        self.lru_cb = self.sb("lru_cb", [128, 2, NRC, 3])
        self.lru_h = self.sb("lru_hc", [128, 2, NRC])
        self.ffn_cb = self.sb("ffn_cb", [128, DEPTH, NFC, 2])
        self.r_lcb = [Reg("lcb%d" % i) for i in range(2)]
        self.r_lh = [Reg("lh%d" % i) for i in range(2)]
        self.r_fcb = [Reg("fcb%d" % i) for i in range(DEPTH)]
        self.Sst = self.sb("Sst", [128, 16, 128])
        self.rS = [Reg("S%d" % i) for i in range(16)]
        self.kcar = self.sb("kcar", [128, 4, 128], BF16)
        self.vcar = self.sb("vcar", [128, 512], BF16)
        self.r_kcar = Reg("kcar")
        self.r_vcar = Reg("vcar")
        self.s_h = self.sb("s_h_sb", [128, 2, NRC, NS])
        self.s_c = self.sb("s_c_sb", [128, 2, NRC, 3, NS])
        self.s_f = self.sb("s_f_sb", [128, DEPTH, NFC, 2, NS])
        self.r_sh = [Reg("sh%d" % i) for i in range(2)]
        self.r_sc = [Reg("sc%d" % i) for i in range(2)]
        self.r_sf = [Reg("sf%d" % i) for i in range(DEPTH)]
        no = self.NORM_OFF
        self.rstd = self.arena[:, no:no + 512]
        self.r_rstd = self.aregs(no, no + 512)
        self.sq = [self.arena[:, no + 512 * (i + 1):no + 512 * (i + 2)] for i in range(2)]
        self.r_sq = [self.aregs(no + 512 * (i + 1), no + 512 * (i + 2)) for i in range(2)]
        self.c1 = self.sb("c1", [128, 2, NRC])
        self.r_c1 = Reg("c1")
        self.hlb = self.sb("hlb", [128, 2, NDC])
        self.r_hlb = Reg("hlb")
        self.esk = self.sb("esk", [128, 32])
        self.r_esk = Reg("esk")
        self.eps_t = self.sb("eps_t", [128, 1])
        self.r_eps = Reg("eps")
        self.op("dve", lambda e: e.memset(self.eps_t[:], EPS), w=[self.r_eps])
        self.one_t = self.sb("one_t", [128, 1])
        self.op("dve", lambda e: e.memset(self.one_t[:], 1.0), w=[self.r_eps])
        self.op("dve", lambda e: e.memset(self.lru_cb[:], 0.0), w=self.r_lcb)
        self.op("dve", lambda e: e.memset(self.lru_h[:], 0.0), w=self.r_lh)
        self.op("dve", lambda e: e.memset(self.ffn_cb[:], 0.0), w=self.r_fcb)
        self.op("dve", lambda e: e.memset(self.Sst[:], 0.0), w=self.rS)

    def P(self, name, c=None, n=1):
        off, ln = self.pp_off[name]
        if c is None:
            return self.pp[:, off:off + ln]
        return self.pp[:, off + c:off + c + n]

    def precompute(self):
        for j in range(2):
            self.op("act", lambda e, j=j: e.activation(out=self.c1[:, j, :], in_=self.P("lru%d_lam" % j), func=AF.Exp, scale=-1.0),
                    r=[self.rpp], w=[self.r_c1])
            self.op("act", lambda e, j=j: e.activation(out=self.c1[:, j, :], in_=self.c1[:, j, :], func=AF.Ln, bias=self.one_t[:]),
                    r=[self.r_c1, self.r_eps], w=[self.r_c1])
            self.op("dve", lambda e, j=j: e.tensor_scalar(out=self.c1[:, j, :], in0=self.c1[:, j, :], scalar1=-8.0, scalar2=None, op0=ALU.mult),
                    r=[self.r_c1], w=[self.r_c1])
        o0 = 2048
        t = self.arena[:, o0:o0 + 64].rearrange("p (l c) -> p l c", l=4)
        den = self.arena[:, o0 + 64:o0 + 80]
        num = self.arena[:, o0 + 80:o0 + 96]
        tr = self.aregs(o0, o0 + 96)
        self.op("act", lambda e: e.activation(out=t, in_=self.P("hg_lbl").rearrange("p (l c) -> p l c", l=4), func=AF.Exp), r=[self.rpp], w=tr)
        self.op("dve", lambda e: e.tensor_tensor(out=num, in0=t[:, 1, :], in1=t[:, 2, :], op=ALU.add), r=tr, w=tr)
        self.op("dve", lambda e: e.tensor_tensor(out=den, in0=t[:, 0, :], in1=t[:, 3, :], op=ALU.add), r=tr, w=tr)
        self.op("dve", lambda e: e.tensor_tensor(out=den, in0=den, in1=num, op=ALU.add), r=tr, w=tr)
        self.op("dve", lambda e: e.reciprocal(out=den, in_=den), r=tr, w=tr)
        self.op("dve", lambda e: e.tensor_tensor(out=self.hlb[:, 0, :], in0=num, in1=den, op=ALU.mult), r=tr, w=[self.r_hlb])
        self.op("dve", lambda e: e.tensor_scalar(out=self.hlb[:, 1, :], in0=self.hlb[:, 0, :], scalar1=-1.0, scalar2=1.0, op0=ALU.mult, op1=ALU.add),
                r=[self.r_hlb], w=[self.r_hlb])
        self.op("act", lambda e: e.activation(out=self.esk[:], in_=self.P("swa_sinks"), func=AF.Exp), r=[self.rpp], w=[self.r_esk])

    def rmsnorm(self, gname, C, out=None, rout=None):
        pst, prg = self.bank()
        for c in range(NDC):
            sq, rsq = self.sq[c % 2].bitcast(BF16), self.r_sq[c % 2]
            self.op("act", lambda e, c=c, sq=sq: e.activation(out=sq[:, 0:C], in_=self.x[:, c, 0:C], func=AF.Square), r=[self.rx[c]], w=rsq)
            self.op("pe", lambda e, c=c, sq=sq: e.matmul(pst[:, 0:C], lhsT=self.onesb, rhs=sq[:, 0:C], start=(c == 0), stop=(c == NDC - 1)),
                    r=rsq + [self.rcstb], w=[prg])
        self.op("act", lambda e: e.activation(out=self.rstd[:, 0:C], in_=pst[:, 0:C], func=AF.Ln, scale=1.0 / D, bias=self.eps_t[:]),
                r=[prg, self.r_eps], w=self.r_rstd)
        self.op("act", lambda e: e.activation(out=self.rstd[:, 0:C], in_=self.rstd[:, 0:C], func=AF.Exp, scale=-0.5), r=self.r_rstd, w=self.r_rstd)
        for c in range(NDC):
            if out is None:
                o_ap, o_r = self.xn[:, c, 0:C], [self.rxn[c]]
            else:
                o_ap, o_r = out(c), rout(c)
            self.op("dve", lambda e, c=c, o_ap=o_ap: e.scalar_tensor_tensor(out=o_ap, in0=self.x[:, c, 0:C], scalar=self.P(gname, c),
                                                                           in1=self.rstd[:, 0:C], op0=ALU.mult, op1=ALU.mult),
                    r=[self.rx[c], self.rpp] + self.r_rstd, w=o_r)

    def rxn_f(self, kc):
        return [self.rxn[kc]]

    def resid_add(self, j, pst, prg, C, bias=None):
        if bias is None:
            self.op("dve", lambda e: e.tensor_tensor(out=self.x[:, j, 0:C], in0=pst[:, 0:C], in1=self.x[:, j, 0:C], op=ALU.add),
                    r=[prg, self.rx[j]], w=[self.rx[j]])
        else:
            self.op("dve", lambda e: e.scalar_tensor_tensor(out=self.x[:, j, 0:C], in0=pst[:, 0:C], scalar=bias, in1=self.x[:, j, 0:C],
                                                           op0=ALU.add, op1=ALU.add),
                    r=[prg, self.rx[j], self.rpp], w=[self.rx[j]])

    def conv_psum(self, mode, pst, prg, taps, bias, cb, rcb, out, rout, C, stt=None, rst=None):
        W = len(taps)
        self.op("dve", lambda e: e.tensor_scalar(out=out, in0=pst[:, 0:C], scalar1=taps[W - 1], scalar2=bias, op0=ALU.mult, op1=ALU.add),
                r=[prg, self.rpp], w=rout)
        if mode == "p":
            for s in range(1, W):
                tp = taps[W - 1 - s]
                self.op("dve", lambda e, s=s, tp=tp: e.scalar_tensor_tensor(out=out[:, s:C], in0=pst[:, 0:C - s], scalar=tp, in1=out[:, s:C],
                                                                             op0=ALU.mult, op1=ALU.add), r=[prg, self.rpp] + rout, w=rout)
                self.op("dve", lambda e, s=s, tp=tp: e.scalar_tensor_tensor(out=out[:, 0:s], in0=cb[:, W - 1 - s:W - 1], scalar=tp, in1=out[:, 0:s],
                                                                             op0=ALU.mult, op1=ALU.add), r=[rcb, self.rpp] + rout, w=rout)
            self.op("act", lambda e: e.copy(out=cb, in_=pst[:, C - (W - 1):C]), r=[prg], w=[rcb])
        else:
            for j in range(W - 1):
                self.op("dve", lambda e, j=j: e.scalar_tensor_tensor(out=out, in0=stt[:, j, :], scalar=taps[j], in1=out, op0=ALU.mult, op1=ALU.add),
                        r=[rst, self.rpp] + rout, w=rout)
            for j in range(W - 2):
                self.op("act", lambda e, j=j: e.copy(out=stt[:, j, :], in_=stt[:, j + 1, :]), r=[rst], w=[rst])
            self.op("act", lambda e: e.copy(out=stt[:, W - 2, :], in_=pst[:, 0:C]), r=[prg], w=[rst])

    def ffn(self, layer, mode, C):
        W_in, W_out = self.w_ffn_in[layer], self.w_ffn_out[layer]
        self.rmsnorm("nffn%d" % layer, C)
        h = Buf(self, 0, NFC, TT, BF16, "ffn_h")
        gact = Buf(self, h.end, 4, TT, BF16, "ffn_gact")
        xc = Buf(self, gact.end, 2, TT, F32, "ffn_xc")
        for g in range(NFC // 4):
            def evac_gate(j, pst, prg, g=g):
                c = g * 4 + j
                xo, xr = xc.ap[:, c % 2, 0:C], xc.r(c % 2)
                taps = [self.P("ffn%d_cw" % layer, k * NFC + c) for k in range(3)]
                self.conv_psum(mode, pst, prg, taps, self.P("ffn%d_cb" % layer, c), self.ffn_cb[:, layer, c, :], self.r_fcb[layer], xo, xr, C,
                               stt=self.s_f[:, layer, c, :, :], rst=self.r_sf[layer])
                self.op("act", lambda e: e.activation(out=gact.ap[:, j, 0:C], in_=xo, func=GELU_F), r=xr, w=gact.r(j))

            self.proj_fm(W_in, NDC, self.xn, self.rxn_f, C, g * 512, 4, evac_gate)

            def evac_up(j, pst, prg, g=g):
                c = g * 4 + j
                self.op("dve", lambda e: e.tensor_tensor(out=h.ap[:, c, 0:C], in0=pst[:, 0:C], in1=gact.ap[:, j, 0:C], op=ALU.mult),
                        r=[prg] + gact.r(j), w=h.r(c))

            self.proj_fm(W_in, NDC, self.xn, self.rxn_f, C, DFF + g * 512, 4, evac_up)
        self.proj_fm(W_out, NFC, h.ap, lambda kc: h.r(kc), C, 0, NDC, lambda j, pst, prg: self.resid_add(j, pst, prg, C))

    def lru(self, layer, mode, C):
        sl = SLOT[layer]
        W_in, W_out, Wa, Wi = self.w_lru_in[sl], self.w_lru_out[sl], self.w_lru_a[sl], self.w_lru_i[sl]
        nm = "lru%d_" % sl
        self.rmsnorm("nmix%d" % layer, C)
        xc = Buf(self, 0, NRC, TT, F32, "lru_xc")
        gg = Buf(self, xc.end, NRC, TT, BF16, "lru_gg")
        xcb = Buf(self, gg.end, 4, TT, BF16, "lru_xcb")
        tm = Buf(self, xcb.end, 6, TT, F32, "lru_tmp")

        def evac_gate(j, pst, prg):
            self.op("act", lambda e: e.activation(out=gg.ap[:, j, 0:C], in_=pst[:, 0:C], func=GELU_F), r=[prg], w=gg.r(j))

        def evac_xr(c, pst, prg):
            taps = [self.P(nm + "cw", k * NRC + c) for k in range(4)]
            self.conv_psum(mode, pst, prg, taps, self.P(nm + "cb", c), self.lru_cb[:, sl, c, :], self.r_lcb[sl], xc.ap[:, c, 0:C], xc.r(c), C,
                           stt=self.s_c[:, sl, c, :, :], rst=self.r_sc[sl])

        self.proj_fm(W_in, NDC, self.xn, self.rxn_f, C, DR, NRC, evac_xr)
        sa, rsa = self.load_band(Wa)
        pin_a = self._last_slab
        si, rsi = self.load_band(Wi)
        pin_i = self._last_slab
        self._slab_pinned.update((pin_a, pin_i))
        cast_done = set()
        bi = 0
        for j0 in range(0, NRC, 2):
            if j0 % 4 == 0:
                g = j0 // 4
                self.proj_fm(W_in, NDC, self.xn, self.rxn_f, C, g * 512, 4, lambda j, pst, prg, g=g: evac_gate(g * 4 + j, pst, prg))
            pend = []
            for j in (j0, j0 + 1):
                for kc in BAND[j]:
                    if kc not in cast_done:
                        cast_done.add(kc)
                        self.op("dve", lambda e, kc=kc: e.tensor_copy(out=xcb.ap[:, kc % 4, 0:C], in_=xc.ap[:, kc, 0:C]), r=xc.r(kc), w=xcb.r(kc % 4))
                pa, rpa = self.bank()
                pi, rpi = self.bank()
                nb = len(BAND[j])
                for (slab, rs, pst, prg) in ((sa, rsa, pa, rpa), (si, rsi, pi, rpi)):
                    for n, kc in enumerate(BAND[j]):
                        b_ = bi + n
                        self.op("pe", lambda e, slab=slab, pst=pst, kc=kc, b=bi + n, n=n, nb=nb: e.matmul(
                            pst[:, 0:C], lhsT=slab[:, b * 128:(b + 1) * 128], rhs=xcb.ap[:, kc % 4, 0:C], start=(n == 0), stop=(n == nb - 1)),
                            r=rs(b_) + xcb.r(kc % 4), w=[prg])
                bi += nb
                pend.append((j, pa, rpa, pi, rpi))
            self._lru_tail2(pend, mode, C, sl, nm, tm, xc, gg)
        self._slab_pinned.difference_update((pin_a, pin_i))
        self.proj_fm(W_out, NRC, gg.ap, lambda kc: gg.r(kc), C, 0, NDC, lambda j, pst, prg: self.resid_add(j, pst, prg, C))

    def _lru_tail2(self, pend, mode, C, sl, nm, tm, xc, gg):
        T3 = []
        for k, (j, pa, rpa, pi, rpi) in enumerate(pend):
            o3 = 3 * k
            T3.append(([tm.ap[:, o3 + q, 0:C] for q in range(3)], [tm.r(o3 + q) for q in range(3)]))
        for k, (j, pa, rpa, pi, rpi) in enumerate(pend):
            (ta, ti_, tm_), (ra, ri, rm) = T3[k]
            self.op("act", lambda e, j=j, pa=pa, ta=ta: e.activation(out=ta, in_=pa[:, 0:C], func=AF.Sigmoid, bias=self.P(nm + "ba", j)),
                    r=[rpa, self.rpp], w=ra)
            self.op("act", lambda e, j=j, pi=pi, ti_=ti_: e.activation(out=ti_, in_=pi[:, 0:C], func=AF.Sigmoid, bias=self.P(nm + "bi", j)),
                    r=[rpi, self.rpp], w=ri)
        for k, (j, pa, rpa, pi, rpi) in enumerate(pend):
            (ta, ti_, tm_), (ra, ri, rm) = T3[k]
            self.op("act", lambda e, j=j, ta=ta: e.activation(out=ta, in_=ta, func=AF.Exp, scale=self.c1[:, sl, j:j + 1]), r=ra + [self.r_c1], w=ra)
            self.op("dve", lambda e, ta=ta, tm_=tm_: e.tensor_tensor(out=tm_, in0=ta, in1=ta, op=ALU.mult), r=ra, w=rm)
            self.op("dve", lambda e, j=j, ti_=ti_: e.tensor_tensor(out=ti_, in0=ti_, in1=xc.ap[:, j, 0:C], op=ALU.mult), r=ri + xc.r(j), w=ri)
        for k, (j, pa, rpa, pi, rpi) in enumerate(pend):
            (ta, ti_, tm_), (ra, ri, rm) = T3[k]
            self.op("act", lambda e, tm_=tm_: e.activation(out=tm_, in_=tm_, func=AF.Sqrt, scale=-1.0, bias=self.one_t[:]), r=rm + [self.r_eps], w=rm)
        for k, (j, pa, rpa, pi, rpi) in enumerate(pend):
            (ta, ti_, tm_), (ra, ri, rm) = T3[k]
            th, rh = tm_, rm
            self.op("dve", lambda e, ti_=ti_, tm_=tm_: e.tensor_tensor(out=ti_, in0=ti_, in1=tm_, op=ALU.mult), r=ri + rm, w=ri)
            if mode == "p":
                self.op("dve", lambda e, j=j, ta=ta, ti_=ti_, th=th: e.tensor_tensor_scan(out=th, data0=ta, data1=ti_, initial=self.lru_h[:, sl, j:j + 1],
                                                                                      op0=ALU.mult, op1=ALU.add), r=ra + ri + [self.r_lh[sl]], w=rh)
                self.op("dve", lambda e, j=j, th=th: e.tensor_copy(out=self.lru_h[:, sl, j:j + 1], in_=th[:, C - 1:C]), r=rh, w=[self.r_lh[sl]])
            else:
                self.op("dve", lambda e, j=j, ta=ta, th=th: e.tensor_tensor(out=th, in0=ta, in1=self.s_h[:, sl, j, :], op=ALU.mult),
                        r=ra + [self.r_sh[sl]], w=rh)
                self.op("dve", lambda e, th=th, ti_=ti_: e.tensor_tensor(out=th, in0=th, in1=ti_, op=ALU.add), r=rh + ri, w=rh)
                self.op("dve", lambda e, j=j, th=th: e.tensor_copy(out=self.s_h[:, sl, j, :], in_=th), r=rh, w=[self.r_sh[sl]])
            self.op("dve", lambda e, j=j, th=th: e.tensor_tensor(out=gg.ap[:, j, 0:C], in0=gg.ap[:, j, 0:C], in1=th, op=ALU.mult),
                    r=gg.r(j) + rh, w=gg.r(j))

    def _lru_tail(self, j, pa, rpa, pi, rpi, mode, C, sl, nm, tm, xc, gg):
        o3 = 3 * (j % 2)
        ta, ti_, tm_ = (tm.ap[:, o3 + k, 0:C] for k in range(3))
        ra, ri, rm = (tm.r(o3 + k) for k in range(3))
        th, rh = tm_, rm
        self.op("act", lambda e, j=j, pa=pa: e.activation(out=ta, in_=pa[:, 0:C], func=AF.Sigmoid, bias=self.P(nm + "ba", j)), r=[rpa, self.rpp], w=ra)
        self.op("act", lambda e, j=j, pi=pi: e.activation(out=ti_, in_=pi[:, 0:C], func=AF.Sigmoid, bias=self.P(nm + "bi", j)), r=[rpi, self.rpp], w=ri)
        self.op("act", lambda e, j=j: e.activation(out=ta, in_=ta, func=AF.Exp, scale=self.c1[:, sl, j:j + 1]), r=ra + [self.r_c1], w=ra)
        self.op("act", lambda e: e.activation(out=tm_, in_=ta, func=AF.Square), r=ra, w=rm)
        self.op("act", lambda e: e.activation(out=tm_, in_=tm_, func=AF.Sqrt, scale=-1.0, bias=self.one_t[:]), r=rm + [self.r_eps], w=rm)
        self.op("dve", lambda e, j=j: e.tensor_tensor(out=ti_, in0=ti_, in1=xc.ap[:, j, 0:C], op=ALU.mult), r=ri + xc.r(j), w=ri)
        self.op("dve", lambda e: e.tensor_tensor(out=ti_, in0=ti_, in1=tm_, op=ALU.mult), r=ri + rm, w=ri)
        if mode == "p":
            self.op("dve", lambda e, j=j: e.tensor_tensor_scan(out=th, data0=ta, data1=ti_, initial=self.lru_h[:, sl, j:j + 1],
                                                               op0=ALU.mult, op1=ALU.add), r=ra + ri + [self.r_lh[sl]], w=rh)
            self.op("act", lambda e, j=j: e.copy(out=self.lru_h[:, sl, j:j + 1], in_=th[:, C - 1:C]), r=rh, w=[self.r_lh[sl]])
        else:
            self.op("dve", lambda e, j=j: e.tensor_tensor(out=th, in0=ta, in1=self.s_h[:, sl, j, :], op=ALU.mult), r=ra + [self.r_sh[sl]], w=rh)
            self.op("dve", lambda e: e.tensor_tensor(out=th, in0=th, in1=ti_, op=ALU.add), r=rh + ri, w=rh)
            self.op("act", lambda e, j=j: e.copy(out=self.s_h[:, sl, j, :], in_=th), r=rh, w=[self.r_sh[sl]])
        self.op("dve", lambda e, j=j: e.tensor_tensor(out=gg.ap[:, j, 0:C], in0=gg.ap[:, j, 0:C], in1=th, op=ALU.mult), r=gg.r(j) + rh, w=gg.r(j))

    def swa(self, layer, mode, C, ti=0, last=False):
        self.rmsnorm("nmix%d" % layer, C)
        qT = Buf(self, 0, 16, TT, BF16, "qT")
        oT = Buf(self, qT.end, 16, TT, BF16, "oT")
        kT = Buf(self, oT.end, 4, TT, BF16, "kT")
        kfp = Buf(self, kT.end, 4, TT, F32, "kfp")
        vd = Buf(self, kfp.end, 4, 512, BF16, "vd")
        vfp = Buf(self, vd.end, 1, 512, F32, "vfp")
        tabs = Buf(self, vfp.end, 2, TT, F32, "tabs")
        bvb = Buf(self, tabs.end, 1, 512, F32, "bvb")
        t2 = Buf(self, bvb.end, 1, TT, F32, "t2")
        pT = Buf(self, t2.end, 2, 512, BF16, "pT")
        rec = Buf(self, pT.end, 1, 512, F32, "rec")
        aux = Buf(self, rec.end, 2, 512, F32, "aux")
        if mode == "p":
            src = self.rope_d[:, :, ti * TT:(ti + 1) * TT]
        else:
            src = self.rope_d[:, :, SEQ:SEQ + NS]
        self.dma(lambda e: e.dma_start(out=tabs.ap[:, :, 0:C], in_=src), w=tabs.all())
        self.dma(lambda e: e.dma_start(out=bvb.ap[:, 0, :], in_=self.bvb_d), w=bvb.all())
        Ct, St = tabs.ap[:, 0, 0:C], tabs.ap[:, 1, 0:C]

        for g0 in range(0, 20, 4):
            def evac_a(j, pst, prg, g0=g0):
                c = g0 + j
                self.op("dve", lambda e: e.scalar_tensor_tensor(out=kfp.ap[:, j, 0:C], in0=pst[:, 0:C], scalar=self.P("swa_bqk", c), in1=Ct,
                                                               op0=ALU.add, op1=ALU.mult), r=[prg, self.rpp] + tabs.all(), w=kfp.r(j))

            self.proj_fm(self.w_qk, NDC, self.xn, self.rxn_f, C, g0 * 128, 4, evac_a)

            def evac_b(j, pst, prg, g0=g0):
                c = g0 + j
                self.op("dve", lambda e: e.scalar_tensor_tensor(out=t2.ap[:, 0, 0:C], in0=pst[:, 0:C], scalar=self.P("swa_bqkr", c), in1=St,
                                                               op0=ALU.add, op1=ALU.mult), r=[prg, self.rpp] + tabs.all(), w=t2.all())
                if c < 16:
                    self.op("dve", lambda e: e.tensor_tensor(out=qT.ap[:, c, 0:C], in0=kfp.ap[:, j, 0:C], in1=t2.ap[:, 0, 0:C], op=ALU.add),
                            r=kfp.r(j) + t2.all(), w=qT.r(c))
                else:
                    g = c - 16
                    self.op("dve", lambda e: e.tensor_tensor(out=kfp.ap[:, j, 0:C], in0=kfp.ap[:, j, 0:C], in1=t2.ap[:, 0, 0:C], op=ALU.add),
                            r=kfp.r(j) + t2.all(), w=kfp.r(j))
                    self.op("act", lambda e: e.copy(out=kT.ap[:, g, 0:C], in_=kfp.ap[:, j, 0:C]), r=kfp.r(j), w=kT.r(g))
                    if mode == "p" and last:
                        self.dma(lambda e: e.dma_start(out=self.o_pk[:, g, :], in_=kfp.ap[:, j, C - 128:C]), r=kfp.r(j))
                    if mode == "s":
                        self.dma(lambda e: e.dma_start(out=self.o_sknew[:, g, :], in_=kfp.ap[:, j, 0:C]), r=kfp.r(j))

            self.proj_fm(self.w_qkr, NDC, self.xn, self.rxn_f, C, g0 * 128, 4, evac_b)

        slab, rs = self.load_slab(self.w_vd, 0, NDC, 0, 512)
        vcur = []
        if mode == "p":
            for blk in range(4):
                pst, prg = self.bank()
                for kc in range(NDC):
                    self.op("pe", lambda e, kc=kc, blk=blk, pst=pst: e.matmul(pst[:, :], lhsT=self.xn[:, kc, blk * 128:(blk + 1) * 128],
                                                                             rhs=slab[:, kc, :], start=(kc == 0), stop=(kc == NDC - 1)),
                            r=rs(kc) + [self.rxn[kc]], w=[prg])
                self.op("dve", lambda e, pst=pst: e.tensor_tensor(out=vfp.ap[:, 0, :], in0=pst[:, :], in1=bvb.ap[:, 0, :], op=ALU.add),
                        r=[prg] + bvb.all(), w=vfp.all())
                self.op("act", lambda e, blk=blk: e.copy(out=vd.ap[:, blk, :], in_=vfp.ap[:, 0, :]), r=vfp.all(), w=vd.r(blk))
                if last and blk == 3:
                    self.dma(lambda e: e.dma_start(out=self.o_pv, in_=vfp.ap[:, 0, :]), r=vfp.all())
        else:
            for s in range(NS):
                pst, prg = self.bank()
                for kc in range(NDC):
                    self.op("pe", lambda e, kc=kc, s=s, pst=pst: e.matmul(pst[0:1, :], lhsT=self.xn[:, kc, s:s + 1], rhs=slab[:, kc, :],
                                                                         start=(kc == 0), stop=(kc == NDC - 1)), r=rs(kc) + [self.rxn[kc]], w=[prg])
                self.op("dve", lambda e, pst=pst: e.tensor_tensor(out=vfp.ap[0:1, 0, :], in0=pst[0:1, :], in1=bvb.ap[0:1, 0, :], op=ALU.add),
                        r=[prg] + bvb.all(), w=vfp.all())
                self.op("act", lambda e, s=s: e.copy(out=vd.ap[0:1, s, :], in_=vfp.ap[0:1, 0, :]), r=vfp.all(), w=vd.r(s))
                self.dma(lambda e, s=s: e.dma_start(out=self.o_svnew[s:s + 1, :], in_=vfp.ap[0:1, 0, :]), r=vfp.all())

        def attend(kp, vp, kc_, vc_, qf, outf, nq, nkc, sample, rd, wr):
            def one(g, par):
                b = 64 * par
                rhs = qf(g, b)
                N = 4 * nq
                blocks = []
                if kp is not None:
                    blocks.append((kp(g, b), vp(g), 128, 0))
                blocks.append((kc_(g, b), vc_(g), nkc, 1))
                for (kap, vap, nk, which) in blocks:
                    pss, rss = self.bank()
                    self.op("pe", lambda e, pss=pss, kap=kap, nk=nk: e.matmul(pss[0:nk, 0:N], lhsT=kap, rhs=rhs, start=True, stop=True),
                            r=rd, w=[rss])
                    self.op("act", lambda e, pss=pss, nk=nk, which=which: e.activation(out=pT.ap[0:nk, which, 0:N], in_=pss[0:nk, 0:N],
                                                                                       func=AF.Exp, scale=0.125), r=[rss], w=pT.r(which))
                    if not sample:
                        msk = self.mprev4 if which == 0 else self.mcur4
                        self.op("dve", lambda e, which=which, msk=msk: e.tensor_tensor(out=pT.ap[:, which, :], in0=pT.ap[:, which, :], in1=msk,
                                                                                       op=ALU.mult), r=pT.r(which) + [self.rcstb], w=pT.r(which))
                    elif which == 0:
                        self.op("dve", lambda e: e.tensor_scalar(out=pT.ap[:, 0, 0:N], in0=pT.ap[:, 0, 0:N], scalar1=self.mprev4[:, 0:1],
                                                                 scalar2=None, op0=ALU.mult), r=pT.r(0) + [self.rcstb], w=pT.r(0))
                pso, rso = self.bank()
                psd, rsd = self.bank()
                nb = len(blocks)
                for n, (kap, vap, nk, which) in enumerate(blocks):
                    self.op("pe", lambda e, vap=vap, nk=nk, which=which, n=n: e.matmul(pso[:, 0:N], lhsT=vap, rhs=pT.ap[0:nk, which, 0:N],
                                                                                     start=(n == 0), stop=(n == nb - 1)),
                            r=rd + pT.r(which), w=[rso])
                for n, (kap, vap, nk, which) in enumerate(blocks):
                    self.op("pe", lambda e, nk=nk, which=which, n=n: e.matmul(psd[:, 0:N], lhsT=self.onesb[0:nk, :], rhs=pT.ap[0:nk, which, 0:N],
                                                                            start=(n == 0), stop=(n == nb - 1)),
                            r=[self.rcstb] + pT.r(which), w=[rsd])
                for hh in range(4):
                    head = 8 * g + 2 * hh + par
                    self.op("act", lambda e, hh=hh, head=head: e.activation(out=rec.ap[b:b + 64, 0, hh * nq:(hh + 1) * nq],
                                                                             in_=psd[b:b + 64, hh * nq:(hh + 1) * nq], func=AF.Ln,
                                                                             bias=self.esk[b:b + 64, head:head + 1]),
                            r=[rsd, self.r_esk], w=rec.all())
                self.op("act", lambda e: e.activation(out=rec.ap[b:b + 64, 0, 0:N], in_=rec.ap[b:b + 64, 0, 0:N], func=AF.Exp, scale=-1.0),
                        r=rec.all(), w=rec.all())
                self.op("dve", lambda e, g=g: e.tensor_tensor(out=outf(g, b), in0=pso[b:b + 64, 0:N].rearrange("p (h q) -> p h q", h=4),
                                                              in1=rec.ap[b:b + 64, 0, 0:N].rearrange("p (h q) -> p h q", h=4), op=ALU.mult),
                        r=[rso] + rec.all(), w=wr(g))


            for g in range(4):
                for par in range(2):
                    one(g, par)

        if mode == "p":
            for qb in range(4):
                qs = slice(qb * 128, (qb + 1) * 128)
                if qb > 0:
                    ps_ = slice((qb - 1) * 128, qb * 128)
                    kp = lambda g, b, ps_=ps_: kT.ap[b:b + 64, g, ps_]
                    vp = lambda g, qb=qb: vd.ap[:, qb - 1, g * 128:(g + 1) * 128]
                elif ti > 0:
                    kp = lambda g, b: self.kcar[b:b + 64, g, :]
                    vp = lambda g: self.vcar[:, g * 128:(g + 1) * 128]
                else:
                    kp = vp = None
                attend(kp, vp,
                       lambda g, b, qs=qs: kT.ap[b:b + 64, g, qs],
                       lambda g, qb=qb: vd.ap[:, qb, g * 128:(g + 1) * 128],
                       lambda g, b, qs=qs: qT.ap[b:b + 64, 4 * g:4 * g + 4, qs],
                       lambda g, b, qs=qs: oT.ap[b:b + 64, 4 * g:4 * g + 4, qs],
                       128, 128, False,
                       qT.all() + kT.all() + vd.all() + [self.r_kcar, self.r_vcar],
                       lambda g: oT.r(4 * g, 4 * g + 4))
            self.op("act", lambda e: e.copy(out=self.kcar[:, :, :], in_=kT.ap[:, :, TT - 128:TT]), r=kT.all(), w=[self.r_kcar])
            self.op("act", lambda e: e.copy(out=self.vcar[:, :], in_=vd.ap[:, 3, :]), r=vd.all(), w=[self.r_vcar])
        else:
            for s in range(NS):
                self.dma(lambda e, s=s: e.dma_start(out=aux.ap[:, 0, :], in_=self.i_skT[s]), w=aux.r(0))
                self.dma(lambda e, s=s: e.dma_start(out=aux.ap[:, 1, :], in_=self.i_svd[s]), w=aux.r(1))
                self.op("act", lambda e: e.copy(out=self.kcar[:, :, :], in_=aux.ap[:, 0, :].rearrange("p (g k) -> p g k", g=4)),
                        r=aux.r(0), w=[self.r_kcar])
                self.op("dve", lambda e: e.tensor_copy(out=self.vcar[:, :], in_=aux.ap[:, 1, :]), r=aux.r(1), w=[self.r_vcar])
                attend(lambda g, b: self.kcar[b:b + 64, g, :],
                       lambda g: self.vcar[:, g * 128:(g + 1) * 128],
                       lambda g, b, s=s: kT.ap[b:b + 64, g, s:s + 1],
                       lambda g, s=s: vd.ap[0:1, s, g * 128:(g + 1) * 128],
                       lambda g, b, s=s: qT.ap[b:b + 64, 4 * g:4 * g + 4, s:s + 1],
                       lambda g, b, s=s: oT.ap[b:b + 64, 4 * g:4 * g + 4, s:s + 1],
                       1, 1, True,
                       qT.all() + kT.all() + vd.all() + [self.r_kcar, self.r_vcar],
                       lambda g: oT.r(4 * g, 4 * g + 4))
                self.dma(lambda e, s=s: e.dma_start(out=self.o_skc[s], in_=self.i_skraw[s, 1:128, :]))
                self.dma(lambda e, s=s: e.dma_start(out=self.o_svc[s], in_=self.i_svraw[s, 1:128, :]))
        self.proj_fm(self.w_swa_o, NDC, oT.ap, lambda kc: oT.r(kc), C, 0, NDC,
                     lambda j, pst, prg: self.resid_add(j, pst, prg, C, bias=self.P("swa_bo", j)))

    def hgrn(self, layer, mode, C, last=False):
        W_in, W_o = self.w_hg_in, self.w_hg_o
        self.rmsnorm("nmix%d" % layer, C)
        vtm = Buf(self, 0, 4, 2048, BF16, "vtm")
        gs = Buf(self, vtm.end, 16, TT, BF16, "gs")
        ebk = Buf(self, gs.end, 4, TT, F32, "ebk")
        kt = Buf(self, ebk.end, 4, TT, BF16, "kt")
        qt = Buf(self, kt.end, 4, TT, BF16, "qt")
        khm = Buf(self, qt.end, 4, 512, BF16, "khm")
        T = Buf(self, khm.end, 6, TT, F32, "hgT")
        at = Buf(self, T.end, 8, 128, BF16, "at")
        Sbf = Buf(self, at.end, 16, 128, BF16, "Sbf")
        lb = lambda c: self.hlb[:, 0, c:c + 1]
        omlb = lambda c: self.hlb[:, 1, c:c + 1]

        if mode == "p":
            for c in range(16):
                self.op("act", lambda e, c=c: e.copy(out=Sbf.ap[:, c, :], in_=self.Sst[:, c, :]), r=[self.rS[c]], w=Sbf.r(c))
            def v_group(cg):
                slab, rs = self.load_slab(W_in, 0, NDC, 4096 + cg * 512, 512)
                for blk in range(4):
                    pst, prg = self.bank()
                    for kc in range(NDC):
                        self.op("pe", lambda e, kc=kc, blk=blk, pst=pst, slab=slab: e.matmul(
                            pst[:, :], lhsT=self.xn[:, kc, blk * 128:(blk + 1) * 128], rhs=slab[:, kc, :], start=(kc == 0), stop=(kc == NDC - 1)),
                            r=rs(kc) + [self.rxn[kc]], w=[prg])
                    self.op("act", lambda e, blk=blk, cg=cg, pst=pst: e.copy(out=vtm.ap[:, blk, cg * 512:(cg + 1) * 512], in_=pst[:, :]),
                            r=[prg], w=vtm.r(blk))
        else:
            vts = self.arena[:, T.off:T.off + 2048]
            rvts = self.aregs(T.off, T.off + 2048)
            for cg in range(4):
                slab, rs = self.load_slab(W_in, 0, NDC, 4096 + cg * 512, 512)
                pst, prg = self.bank()
                for kc in range(NDC):
                    self.op("pe", lambda e, kc=kc, pst=pst, slab=slab: e.matmul(pst[0:NS, :], lhsT=self.xn[:, kc, 0:NS], rhs=slab[:, kc, :],
                                                                               start=(kc == 0), stop=(kc == NDC - 1)), r=rs(kc) + [self.rxn[kc]], w=[prg])
                self.op("act", lambda e, cg=cg, pst=pst: e.copy(out=vts[0:NS, cg * 512:(cg + 1) * 512], in_=pst[0:NS, :]), r=[prg], w=rvts)

        def evac_g(j, pst, prg):
            self.op("act", lambda e: e.activation(out=gs.ap[:, j, 0:C], in_=pst[:, 0:C], func=AF.Silu), r=[prg], w=gs.r(j))

        if mode != "p":
            self.proj_fm(W_in, NDC, self.xn, self.rxn_f, C, 6144, 16, evac_g)

        if mode == "p":
            f_, lf, bb, enb, kh, qf = (T.ap[:, k, 0:C] for k in range(6))
            rf, rlf, rbb, renb, rkh, rqf = (T.r(k) for k in range(6))
            for hg in range(4):
                zb = []

                def evac_z(j, pst, prg, hg=hg, zb=zb):
                    zb.append((j, pst, prg))
                    self._reserved.add(self.ps.index(pst))

                self.proj_fm(W_in, NDC, self.xn, self.rxn_f, C, 2048 + hg * 512, 4, evac_z)
                held = set(self.ps.index(z[1]) for z in zb)
                self.proj_fm(W_in, NDC, self.xn, self.rxn_f, C, 6144 + hg * 512, 4, lambda j, pst, prg, hg=hg: evac_g(hg * 4 + j, pst, prg))
                self._reserved.update(held)
                v_group(hg)
                self.hg_zstage(hg, C, zb, ebk, kt, khm, T)

                def evac_q(j, pst, prg, hg=hg):
                    self.op("act", lambda e: e.activation(out=qf, in_=pst[:, 0:C], func=AF.Silu), r=[prg], w=rqf)
                    self.op("dve", lambda e: e.tensor_tensor(out=qt.ap[:, j, 0:C], in0=qf, in1=ebk.ap[:, j, 0:C], op=ALU.mult),
                            r=rqf + ebk.r(j), w=qt.r(j))

                self.proj_fm(W_in, NDC, self.xn, self.rxn_f, C, hg * 512, 4, evac_q)
                self.hg_chunks(hg, C, last, vtm, gs, ebk, kt, qt, khm, T, at, Sbf)
        else:
            fz = self.arena[:, ebk.off:ebk.off + 64].rearrange("p (c s) -> p c s", c=16)
            kz = self.arena[:, ebk.off + 64:ebk.off + 128].rearrange("p (c s) -> p c s", c=16)
            qz = self.arena[:, ebk.off + 128:ebk.off + 192].rearrange("p (c s) -> p c s", c=16)
            oz = self.arena[:, ebk.off + 192:ebk.off + 256].rearrange("p (c s) -> p c s", c=16)
            rz = self.aregs(ebk.off, ebk.off + 256)

            def evac_zs(c, pst, prg):
                self.op("act", lambda e: e.activation(out=fz[:, c, :], in_=pst[:, 0:C], func=AF.Sigmoid), r=[prg], w=rz)
                self.op("dve", lambda e: e.tensor_scalar(out=fz[:, c, :], in0=fz[:, c, :], scalar1=omlb(c), scalar2=lb(c), op0=ALU.mult, op1=ALU.add),
                        r=rz + [self.r_hlb], w=rz)
                self.op("dve", lambda e: e.tensor_scalar(out=kz[:, c, :], in0=fz[:, c, :], scalar1=-1.0, scalar2=1.0, op0=ALU.mult, op1=ALU.add),
                        r=rz, w=rz)

            self.proj_fm(W_in, NDC, self.xn, self.rxn_f, C, 2048, 16, evac_zs)

            def evac_qs(c, pst, prg):
                self.op("act", lambda e: e.activation(out=qz[:, c, :], in_=pst[:, 0:C], func=AF.Silu), r=[prg], w=rz)

            self.proj_fm(W_in, NDC, self.xn, self.rxn_f, C, 0, 16, evac_qs)
            Sl = Buf(self, kt.off, 4, 128, F32, "Sl")
            tq = Buf(self, Sl.end, 2, 128, F32, "tq")
            n = 0
            for s in range(NS):
                for c in range(16):
                    k = n % 2
                    n += 1
                    self.dma(lambda e, s=s, c=c, k=k: e.dma_start(out=Sl.ap[:, k, :], in_=self.i_sS[s, c]), w=Sl.r(k))
                    pv, rpv = self.bank()
                    self.op("pe", lambda e, s=s, c=c, pv=pv: e.matmul(pv[:, 0:128], lhsT=self.sel[0:NS, s * 128:(s + 1) * 128],
                                                                     rhs=vts[0:NS, c * 128:(c + 1) * 128], start=True, stop=True),
                            r=[self.rcst] + rvts, w=[rpv])
                    self.op("dve", lambda e, s=s, c=c, k=k, pv=pv: e.tensor_scalar(out=tq.ap[:, k, :], in0=pv[:, 0:128], scalar1=kz[:, c, s:s + 1],
                                                                                  scalar2=None, op0=ALU.mult), r=[rpv] + rz, w=tq.r(k))
                    self.op("dve", lambda e, s=s, c=c, k=k: e.scalar_tensor_tensor(out=Sl.ap[:, 2 + k, :], in0=Sl.ap[:, k, :], scalar=fz[:, c, s:s + 1],
                                                                                  in1=tq.ap[:, k, :], op0=ALU.mult, op1=ALU.add),
                            r=Sl.r(k) + tq.r(k) + rz, w=Sl.r(2 + k))
                    self.dma(lambda e, s=s, c=c, k=k: e.dma_start(out=self.o_sS[s, c], in_=Sl.ap[:, 2 + k, :]), r=Sl.r(2 + k))
                    po, rpo = self.bank()
                    self.op("pe", lambda e, s=s, c=c, k=k, po=po: e.matmul(po[:, 0:1], lhsT=Sl.ap[:, 2 + k, :], rhs=qz[:, c, s:s + 1],
                                                                          start=True, stop=True), r=Sl.r(2 + k) + rz, w=[rpo])
                    self.op("act", lambda e, s=s, c=c, po=po: e.copy(out=oz[:, c, s:s + 1], in_=po[:, 0:1]), r=[rpo], w=rz)
            ozf = self.arena[:, ebk.off + 192:ebk.off + 256]
            osq = T.ap[:, 4, 0:64]
            rsn = T.ap[:, 5, 0:64]
            pn, rpn = self.bank()
            self.op("act", lambda e: e.activation(out=osq, in_=ozf, func=AF.Square), r=rz, w=T.r(4))
            self.op("pe", lambda e: e.matmul(pn[:, 0:64], lhsT=self.ones, rhs=osq, start=True, stop=True), r=T.r(4) + [self.rcst], w=[rpn])
            self.op("act", lambda e: e.activation(out=rsn, in_=pn[:, 0:64], func=AF.Sqrt, scale=1.0 / 128, bias=self.eps_t[:]),
                    r=[rpn, self.r_eps], w=T.r(5))
            self.op("dve", lambda e: e.reciprocal(out=rsn, in_=rsn), r=T.r(5), w=T.r(5))
            self.op("dve", lambda e: e.scalar_tensor_tensor(out=osq, in0=ozf, scalar=self.P("hg_norm", 0), in1=rsn, op0=ALU.mult, op1=ALU.mult),
                    r=rz + T.r(5) + [self.rpp], w=T.r(4))
            self.op("dve", lambda e: e.tensor_tensor(out=gs.ap[:, :, 0:NS], in0=osq.rearrange("p (c s) -> p c s", c=16), in1=gs.ap[:, :, 0:NS],
                                                     op=ALU.mult), r=T.r(4) + gs.all(), w=gs.all())
        self.proj_fm(W_o, NDC, gs.ap, lambda kc: gs.r(kc), C, 0, NDC, lambda j, pst, prg: self.resid_add(j, pst, prg, C))

    def hg_zstage(self, hg, C, zb, ebk, kt, khm, T):
        lb = lambda c: self.hlb[:, 0, c:c + 1]
        omlb = lambda c: self.hlb[:, 1, c:c + 1]
        for p0 in (0, 2):
            hs = []
            for k in range(2):
                j, pst, prg = zb[p0 + k]
                hs.append((j, hg * 4 + j, pst, prg, [T.ap[:, 3 * k + q, 0:C] for q in range(3)], [T.r(3 * k + q) for q in range(3)]))
            for (j, c, pst, prg, (A, B, Cc), (rA, rB, rC)) in hs:
                self.op("act", lambda e, A=A, pst=pst: e.activation(out=A, in_=pst[:, 0:C], func=AF.Sigmoid), r=[prg], w=rA)
                self._reserved.discard(self.ps.index(pst))
            for (j, c, pst, prg, (A, B, Cc), (rA, rB, rC)) in hs:
                self.op("dve", lambda e, A=A, c=c: e.tensor_scalar(out=A, in0=A, scalar1=omlb(c), scalar2=lb(c), op0=ALU.mult, op1=ALU.add),
                        r=rA + [self.r_hlb], w=rA)
            for (j, c, pst, prg, (A, B, Cc), (rA, rB, rC)) in hs:
                self.op("act", lambda e, A=A, B=B: e.activation(out=B, in_=A, func=AF.Ln), r=rA, w=rB)
            for (j, c, pst, prg, (A, B, Cc), (rA, rB, rC)) in hs:
                self.op("dve", lambda e, B=B, Cc=Cc: e.tensor_tensor_scan(out=Cc, data0=self.m01[:, 0:C], data1=B, initial=0.0, op0=ALU.mult, op1=ALU.add),
                        r=rB + [self.rcst], w=rC)
            for (j, c, pst, prg, (A, B, Cc), (rA, rB, rC)) in hs:
                self.op("act", lambda e, Cc=Cc, j=j: e.activation(out=ebk.ap[:, j, 0:C], in_=Cc, func=AF.Exp), r=rC, w=ebk.r(j))
                self.op("act", lambda e, Cc=Cc, B=B: e.activation(out=B, in_=Cc, func=AF.Exp, scale=-1.0), r=rC, w=rB)
            for (j, c, pst, prg, (A, B, Cc), (rA, rB, rC)) in hs:
                self.op("dve", lambda e, A=A: e.tensor_scalar(out=A, in0=A, scalar1=-1.0, scalar2=1.0, op0=ALU.mult, op1=ALU.add), r=rA, w=rA)
                self.op("dve", lambda e, A=A, B=B: e.tensor_tensor(out=A, in0=A, in1=B, op=ALU.mult), r=rA + rB, w=rA)
            for (j, c, pst, prg, (A, B, Cc), (rA, rB, rC)) in hs:
                self.op("act", lambda e, A=A, j=j: e.copy(out=kt.ap[:, j, 0:C], in_=A), r=rA, w=kt.r(j))
            for (j, c, pst, prg, (A, B, Cc), (rA, rB, rC)) in hs:
                for ch in range(C // 64):
                    cs = slice(ch * 64, (ch + 1) * 64)
                    lastc = ch * 64 + 63
                    self.op("dve", lambda e, cs=cs, lastc=lastc, A=A, Cc=Cc, j=j: e.tensor_scalar(
                        out=Cc[:, cs], in0=A[:, cs], scalar1=ebk.ap[:, j, lastc:lastc + 1], scalar2=None, op0=ALU.mult),
                        r=rA + ebk.r(j), w=rC)
            for blk in range(4):
                for (j, c, pst, prg, (A, B, Cc), (rA, rB, rC)) in hs:
                    pt_, rpt = self.bank()
                    self.op("pe", lambda e, blk=blk, pt_=pt_, Cc=Cc: e.transpose(pt_[:, 0:128], Cc[:, blk * 128:(blk + 1) * 128], self.ident),
                            r=rC + [self.rcst], w=[rpt])
                    self.op("act", lambda e, blk=blk, pt_=pt_, j=j: e.copy(out=khm.ap[:, j, blk * 128:(blk + 1) * 128], in_=pt_[:, 0:128]),
                            r=[rpt], w=khm.r(j))

    def hg_chunks(self, hg, C, last, vtm, gs, ebk, kt, qt, khm, T, at, Sbf):
        osb = [(T.ap[:, k, 0:C], T.r(k)) for k in range(4)]
        for blk in range(4):
            cs = slice(blk * 128, (blk + 1) * 128)
            pos = []
            for j in range(4):
                pos.append(self.bank())
                self._reserved.add(self.ps.index(pos[j][0]))
            for j in range(4):
                c = hg * 4 + j
                po, rpo = pos[j]
                pa, rpa = self.bank()
                self.op("pe", lambda e, cs=cs, pa=pa, j=j: e.matmul(pa[:, 0:128], lhsT=kt.ap[:, j, cs], rhs=qt.ap[:, j, cs], start=True, stop=True),
                        r=kt.r(j) + qt.r(j), w=[rpa])
                a_ap, a_r = at.ap[:, 2 * j + blk % 2, :], at.r(2 * j + blk % 2)
                self.op("dve", lambda e, pa=pa, a_ap=a_ap: e.tensor_tensor(out=a_ap, in0=pa[:, 0:128], in1=self.mhg, op=ALU.mult),
                        r=[rpa, self.rcstb], w=a_r)
                self.op("pe", lambda e, blk=blk, po=po, a_ap=a_ap, c=c: e.matmul(po[:, 0:128], lhsT=vtm.ap[:, blk, c * 128:(c + 1) * 128], rhs=a_ap,
                                                                                start=True, stop=False), r=vtm.r(blk) + a_r, w=[rpo])
            for half in range(2):
                hs = slice(blk * 128 + half * 64, blk * 128 + half * 64 + 64)
                lastc = blk * 128 + half * 64 + 63
                pSs = []
                for j in range(4):
                    c = hg * 4 + j
                    po, rpo = pos[j]
                    self.op("pe", lambda e, half=half, hs=hs, po=po, c=c, j=j: e.matmul(po[:, half * 64:(half + 1) * 64], lhsT=Sbf.ap[:, c, :],
                                                                                       rhs=qt.ap[:, j, hs], start=False, stop=(half == 1)),
                            r=Sbf.r(c) + qt.r(j), w=[rpo])
                    pS, rpS = self.bank()
                    pSs.append((pS, rpS))
                    self.op("pe", lambda e, half=half, blk=blk, pS=pS, c=c, j=j: e.matmul(
                        pS[:, 0:128], lhsT=khm.ap[half * 64:(half + 1) * 64, j, blk * 128:(blk + 1) * 128],
                        rhs=vtm.ap[half * 64:(half + 1) * 64, blk, c * 128:(c + 1) * 128], start=True, stop=True),
                        r=khm.r(j) + vtm.r(blk), w=[rpS])
                for j in range(4):
                    c = hg * 4 + j
                    pS, rpS = pSs[j]
                    self.op("dve", lambda e, pS=pS, lastc=lastc, c=c, j=j: e.scalar_tensor_tensor(
                        out=self.Sst[:, c, :], in0=self.Sst[:, c, :], scalar=ebk.ap[:, j, lastc:lastc + 1], in1=pS[:, 0:128],
                        op0=ALU.mult, op1=ALU.add), r=[self.rS[c], rpS] + ebk.r(j), w=[self.rS[c]])
                for j in range(4):
                    c = hg * 4 + j
                    self.op("act", lambda e, c=c: e.copy(out=Sbf.ap[:, c, :], in_=self.Sst[:, c, :]), r=[self.rS[c]], w=Sbf.r(c))
            for j in range(4):
                po, rpo = pos[j]
                self._reserved.discard(self.ps.index(po))
                self.op("act", lambda e, cs=cs, po=po, j=j: e.copy(out=osb[j][0][:, cs], in_=po[:, 0:128]), r=[rpo], w=osb[j][1])
        for j in range(4):
            c = hg * 4 + j
            self.head_norm(osb[j][0], osb[j][1], T.ap[:, 5, 0:C], T.r(5), gs, c, C, T.ap[:, 4, 0:C], T.r(4))
            if last:
                self.dma(lambda e, c=c: e.dma_start(out=self.o_pS[c], in_=self.Sst[:, c, :]), r=[self.rS[c]])

    def head_norm(self, osb, rosb, osq, rosq, gs, c, C, tmp, rtmp):
        pn, rpn = self.bank()
        self.op("act", lambda e: e.activation(out=osq, in_=osb, func=AF.Square), r=rosb, w=rosq)
        self.op("pe", lambda e: e.matmul(pn[:, 0:C], lhsT=self.ones, rhs=osq, start=True, stop=True), r=rosq + [self.rcst], w=[rpn])
        self.op("act", lambda e: e.activation(out=tmp, in_=pn[:, 0:C], func=AF.Sqrt, scale=1.0 / 128, bias=self.eps_t[:]),
                r=[rpn, self.r_eps], w=rtmp)
        self.op("dve", lambda e: e.reciprocal(out=tmp, in_=tmp), r=rtmp, w=rtmp)
        self.op("dve", lambda e: e.scalar_tensor_tensor(out=osq, in0=osb, scalar=self.P("hg_norm", 0), in1=tmp, op0=ALU.mult, op1=ALU.mult),
                r=rosb + rtmp + [self.rpp], w=rosq)
        self.op("dve", lambda e: e.tensor_tensor(out=gs.ap[:, c, 0:C], in0=osq, in1=gs.ap[:, c, 0:C], op=ALU.mult), r=rosq + gs.r(c), w=gs.r(c))

    def build(self, pp_off, npp, nlayers=DEPTH, do_prompt=True, do_sample=True, ntiles=NT):
        self.setup(pp_off, npp)
        self.xT = self.inp("xT", [D, SEQ])
        self.xsT = self.inp("xsT", [D, NS])
        self.w_lru_in = [self.inp("lru_w_in%d" % i, [D, 2 * DR]) for i in range(2)]
        self.w_lru_out = [self.inp("lru_w_out%d" % i, [DR, D]) for i in range(2)]
        self.w_lru_a = [self.inp("lru_wa%d" % i, [128, 64 * 128]) for i in range(2)]
        self.w_lru_i = [self.inp("lru_wi%d" % i, [128, 64 * 128]) for i in range(2)]
        self.w_qk = self.inp("swa_wqk", [D, 2560])
        self.w_qkr = self.inp("swa_wqkr", [D, 2560])
        self.w_vd = self.inp("swa_wvd", [D, 512])
        self.w_swa_o = self.inp("swa_wo", [D, D])
        self.w_hg_in = self.inp("hg_w_in", [D, 8192])
        self.w_hg_o = self.inp("hg_w_o", [D, D])
        self.w_ffn_in = [self.inp("ffn_w_in%d" % i, [D, 2 * DFF]) for i in range(DEPTH)]
        self.w_ffn_out = [self.inp("ffn_w_out%d" % i, [DFF, D]) for i in range(DEPTH)]
        self.rope_d = self.inp("rope", [128, 2, SEQ + NS])
        self.bvb_d = self.inp("bvb", [128, 512])
        self.i_sh = self.inp("s_h", [128, 2 * NRC * NS])
        self.i_sc = self.inp("s_c", [128, 2 * NRC * 3 * NS])
        self.i_sf = self.inp("s_f", [128, DEPTH * NFC * 2 * NS])
        self.i_skT = self.inp("s_kT", [NS, 128, 512])
        self.i_svd = self.inp("s_vd", [NS, 128, 512])
        self.i_skraw = self.inp("s_kraw", [NS, 128, 256])
        self.i_svraw = self.inp("s_vraw", [NS, 128, 256])
        self.i_sS = self.inp("s_S", [NS, 16, 128, 128])
        self.o_yT = self.outp("yT", [D, SEQ])
        self.o_ysT = self.outp("ysT", [D, NS])
        self.o_plh = self.outp("p_lru_h", [128, 2 * NRC])
        self.o_plc = self.outp("p_lru_c", [128, 2 * NRC * 3])
        self.o_pk = self.outp("p_k", [128, 4, 128])
        self.o_pv = self.outp("p_v", [128, 512])
        self.o_pS = self.outp("p_S", [16, 128, 128])
        self.o_pf = self.outp("p_f", [128, DEPTH * NFC * 2])
        self.o_slh = self.outp("o_s_h", [128, 2 * NRC * NS])
        self.o_slc = self.outp("o_s_c", [128, 2 * NRC * 3 * NS])
        self.o_sf = self.outp("o_s_f", [128, DEPTH * NFC * 2 * NS])
        self.o_skc = self.outp("o_s_kc", [NS, 127, 256])
        self.o_svc = self.outp("o_s_vc", [NS, 127, 256])
        self.o_sknew = self.outp("o_s_knew", [128, 4, NS])
        self.o_svnew = self.outp("o_s_vnew", [NS, 512])
        self.o_sS = self.outp("o_s_S", [NS, 16, 128, 128])
        self.precompute()

        def layers(mode, C, ti, last):
            for layer in range(nlayers):
                kind = MIX[layer]
                if getattr(self, "skip_mix", False):
                    pass
                elif kind == 0:
                    self.lru(layer, mode, C)
                elif kind == 1:
                    self.swa(layer, mode, C, ti=ti, last=last)
                else:
                    self.hgrn(layer, mode, C, last=last)
                if not getattr(self, "skip_ffn", False):
                    self.ffn(layer, mode, C)
            yb = Buf(self, 0, NDC, TT, F32, "ybuf")
            self.rmsnorm("nfinal", C, out=lambda c: yb.ap[:, c, 0:C], rout=lambda c: yb.r(c))
            return yb

        if do_prompt:
            for ti in range(ntiles):
                last = (ti == ntiles - 1)
                for c in range(NDC):
                    self.dma(lambda e, c=c, ti=ti: e.dma_start(out=self.x[:, c, :], in_=self.xT[c * 128:(c + 1) * 128, ti * TT:(ti + 1) * TT]),
                             w=[self.rx[c]])
                yb = layers("p", TT, ti, last)
                for c in range(NDC):
                    self.dma(lambda e, c=c, ti=ti, yb=yb: e.dma_start(out=self.o_yT[c * 128:(c + 1) * 128, ti * TT:(ti + 1) * TT], in_=yb.ap[:, c, :]),
                             r=yb.r(c))
            self.dma(lambda e: e.dma_start(out=self.o_plh, in_=self.lru_h[:].rearrange("p a b -> p (a b)")), r=self.r_lh)
            self.dma(lambda e: e.dma_start(out=self.o_plc, in_=self.lru_cb[:].rearrange("p a b c -> p (a b c)")), r=self.r_lcb)
            self.dma(lambda e: e.dma_start(out=self.o_pf, in_=self.ffn_cb[:].rearrange("p a b c -> p (a b c)")), r=self.r_fcb)
        if do_sample:
            self.dma(lambda e: e.dma_start(out=self.s_h[:].rearrange("p a b c -> p (a b c)"), in_=self.i_sh), w=self.r_sh)
            self.dma(lambda e: e.dma_start(out=self.s_c[:].rearrange("p a b c d -> p (a b c d)"), in_=self.i_sc), w=self.r_sc)
            self.dma(lambda e: e.dma_start(out=self.s_f[:].rearrange("p a b c d -> p (a b c d)"), in_=self.i_sf), w=self.r_sf)
            for c in range(NDC):
                self.dma(lambda e, c=c: e.dma_start(out=self.x[:, c, 0:NS], in_=self.xsT[c * 128:(c + 1) * 128, :]), w=[self.rx[c]])
            yb = layers("s", NS, 0, False)
            for c in range(NDC):
                self.dma(lambda e, c=c, yb=yb: e.dma_start(out=self.o_ysT[c * 128:(c + 1) * 128, :], in_=yb.ap[:, c, 0:NS]), r=yb.r(c))
            self.dma(lambda e: e.dma_start(out=self.o_slh, in_=self.s_h[:].rearrange("p a b c -> p (a b c)")), r=self.r_sh)
            self.dma(lambda e: e.dma_start(out=self.o_slc, in_=self.s_c[:].rearrange("p a b c d -> p (a b c d)")), r=self.r_sc)
            self.dma(lambda e: e.dma_start(out=self.o_sf, in_=self.s_f[:].rearrange("p a b c d -> p (a b c d)")), r=self.r_sf)
        self.S.emit(self.st)
        self.st.close()
        return self.nc
def make_shared(inp):
    sh = {}
    pp = PP()
    for l in range(DEPTH):
        pp.add("nmix%d" % l, fm(inp["norm_mix"][l]))
        pp.add("nffn%d" % l, fm(inp["norm_ffn"][l]))
    pp.add("nfinal", fm(inp["norm_final"]))
    for j in range(2):
        pp.add("lru%d_cw" % j, fm(inp["lru_conv_w"][j]))
        pp.add("lru%d_cb" % j, fm(inp["lru_conv_b"][j]))
        pp.add("lru%d_ba" % j, fm(inp["lru_b_a"][j]))
        pp.add("lru%d_bi" % j, fm(inp["lru_b_i"][j]))
        pp.add("lru%d_lam" % j, fm(inp["lru_lambda"][j]))
        sh["lru_w_in%d" % j] = np.ascontiguousarray(inp["lru_w_in"][j])
        sh["lru_w_out%d" % j] = np.ascontiguousarray(inp["lru_w_out"][j])
        sh["lru_wa%d" % j] = band_layout(inp["lru_w_a"][j])
        sh["lru_wi%d" % j] = band_layout(inp["lru_w_i"][j])
    cols, rot = qk_cols()
    vc = v_cols()
    wqkv = inp["swa_w_qkv"][0]
    bqkv = inp["swa_b_qkv"][0]
    sh["swa_wqk"] = np.ascontiguousarray(wqkv[:, cols])
    sh["swa_wqkr"] = np.ascontiguousarray(wqkv[:, rot])
    sh["swa_wvd"] = np.ascontiguousarray(wqkv[:, vc])
    sh["swa_wo"] = np.ascontiguousarray(inp["swa_w_o"][0])
    pp.add("swa_bqk", fm(bqkv[cols]))
    pp.add("swa_bqkr", fm(bqkv[rot]))
    pp.add("swa_bo", fm(inp["swa_b_o"][0]))
    pp.add("swa_sinks", np.tile(np.asarray(inp["swa_sinks"][0], np.float32)[None, :], (128, 1)))
    sh["bvb"] = np.ascontiguousarray(np.tile(np.asarray(bqkv[vc], np.float32)[None, :], (128, 1)))
    sh["hg_w_in"] = np.ascontiguousarray(inp["hg_w_in"][0])
    sh["hg_w_o"] = np.ascontiguousarray(inp["hg_w_o"][0])
    pp.add("hg_lbl", fm(inp["hg_lb_logits"]))
    pp.add("hg_norm", fm(inp["hg_norm"][0]))
    for l in range(DEPTH):
        pp.add("ffn%d_cw" % l, fm(inp["ffn_conv_w"][l]))
        pp.add("ffn%d_cb" % l, fm(inp["ffn_conv_b"][l]))
        sh["ffn_w_in%d" % l] = np.ascontiguousarray(inp["ffn_w_in"][l])
        sh["ffn_w_out%d" % l] = np.ascontiguousarray(inp["ffn_w_out"][l])
    sh["pp"] = pp.array()
    cst = np.zeros((128, NCST), np.float32)
    cst[:, 0:128] = np.eye(128, dtype=np.float32)
    cst[:, 128:256] = 1.0
    m01 = np.ones((512,), np.float32)
    m01[::64] = 0.0
    cst[:, 256:768] = m01[None, :]
    for s in range(NS):
        cst[s, 768 + s * 128:768 + (s + 1) * 128] = 1.0
    sh["cst"] = cst
    j = np.arange(128)[:, None]
    i = np.arange(128)[None, :]
    msk = np.zeros((128, NMSK), np.float32)
    msk[:, 0:512] = np.tile((j > i).astype(np.float32), (1, 4))
    msk[:, 512:1024] = np.tile((j <= i).astype(np.float32), (1, 4))
    msk[:, 1024:1152] = ((j // 64 == i // 64) & (j <= i)).astype(np.float32)
    sh["msk"] = msk
    Cp, Sp = rope_tables(np.arange(SEQ))
    Cs, Ss = rope_tables(np.full((NS,), PAST))
    rope = np.zeros((128, 2, SEQ + NS), np.float32)
    rope[:, 0, :SEQ] = Cp
    rope[:, 1, :SEQ] = Sp
    rope[:, 0, SEQ:] = Cs
    rope[:, 1, SEQ:] = Ss
    sh["rope"] = rope
    return sh, pp.off, pp.n


def make_core(inp, b, s0):
    d = {}
    d["xT"] = np.ascontiguousarray(np.asarray(inp["x_prompt"][b], np.float32).T)
    d["xsT"] = np.ascontiguousarray(np.asarray(inp["x_sample"][s0:s0 + NS, 0, :], np.float32).T)
    sl = slice(s0, s0 + NS)
    h = fm(inp["state_lru_h"][:, sl])
    d["s_h"] = np.ascontiguousarray(h.transpose(0, 1, 3, 2)).reshape(128, -1)
    c = fm(inp["state_lru_conv"][:, sl])
    d["s_c"] = np.ascontiguousarray(c.transpose(0, 1, 4, 3, 2)).reshape(128, -1)
    f = fm(inp["state_ffn_conv"][:, sl])
    d["s_f"] = np.ascontiguousarray(f.transpose(0, 1, 4, 3, 2)).reshape(128, -1)
    ck = np.asarray(inp["cache_swa_k"][0, sl], np.float32)
    cv = np.asarray(inp["cache_swa_v"][0, sl], np.float32)
    kT = ck.transpose(0, 3, 2, 1)
    d["s_kT"] = np.ascontiguousarray(np.concatenate([kT, kT], axis=1)).reshape(NS, 128, 512)
    d["s_vd"] = np.ascontiguousarray(np.concatenate([cv, cv], axis=3)).reshape(NS, 128, 512)
    d["s_kraw"] = np.ascontiguousarray(ck.reshape(NS, 128, 256))
    d["s_vraw"] = np.ascontiguousarray(cv.reshape(NS, 128, 256))
    d["s_S"] = np.ascontiguousarray(np.asarray(inp["state_hgrn"][0, sl], np.float32))
    return d


def unpack_core(r):
    o = {}
    o["y_p"] = r["yT"].T
    o["y_s"] = r["ysT"].T
    o["p_h"] = r["p_lru_h"].reshape(128, 2, NRC).transpose(1, 2, 0).reshape(2, DR)
    o["p_c"] = r["p_lru_c"].reshape(128, 2, NRC, 3).transpose(1, 3, 2, 0).reshape(2, 3, DR)
    o["p_k"] = r["p_k"][0:64].transpose(2, 1, 0)
    o["p_v"] = r["p_v"].reshape(128, 4, 128)[:, :, 0:64]
    o["p_S"] = r["p_S"]
    o["p_f"] = r["p_f"].reshape(128, DEPTH, NFC, 2).transpose(1, 3, 2, 0).reshape(DEPTH, 2, DFF)
    o["s_h"] = r["o_s_h"].reshape(128, 2, NRC, NS).transpose(1, 3, 2, 0).reshape(2, NS, DR)
    o["s_c"] = r["o_s_c"].reshape(128, 2, NRC, 3, NS).transpose(1, 4, 3, 2, 0).reshape(2, NS, 3, DR)
    knew = r["o_s_knew"][0:64].transpose(2, 1, 0)
    o["s_k"] = np.concatenate([r["o_s_kc"].reshape(NS, 127, 4, 64), knew[:, None]], axis=1)
    vnew = r["o_s_vnew"].reshape(NS, 4, 128)[:, :, 0:64]
    o["s_v"] = np.concatenate([r["o_s_vc"].reshape(NS, 127, 4, 64), vnew[:, None]], axis=1)
    o["s_S"] = r["o_s_S"]
    o["s_f"] = r["o_s_f"].reshape(128, DEPTH, NFC, 2, NS).transpose(1, 4, 3, 2, 0).reshape(DEPTH, NS, 2, DFF)
    return o


N_CORES = 8
PROMPT_CORE = (0, 2, 4, 6)


def kernel(**inputs):
    inp = {k: np.asarray(v) for k, v in inputs.items()}
    sh, pp_off, npp = make_shared(inp)
    nc = Builder().build(pp_off, npp)
    in_maps = []
    zero_xT = np.zeros((D, SEQ), np.float32)
    for i in range(N_CORES):
        m = dict(sh)
        m.update(make_core(inp, PROMPT_CORE.index(i) if i in PROMPT_CORE else 0, NS * i))
        if i not in PROMPT_CORE:
            m["xT"] = zero_xT
        in_maps.append(m)
    res = run_bass_kernel_spmd(nc, in_maps, core_ids=list(range(N_CORES)))
    outs = [unpack_core(r) for r in res.results]
    f32 = np.float32
    y_prompt = np.stack([outs[PROMPT_CORE[b]]["y_p"] for b in range(4)]).astype(f32)
    y_sample = np.concatenate([outs[i]["y_s"] for i in range(N_CORES)])[:, None, :].astype(f32)
    p_h = np.stack([outs[PROMPT_CORE[b]]["p_h"] for b in range(4)], axis=1).astype(f32)
    p_c = np.stack([outs[PROMPT_CORE[b]]["p_c"] for b in range(4)], axis=1).astype(f32)
    p_k = np.stack([outs[PROMPT_CORE[b]]["p_k"] for b in range(4)])[None].astype(f32)
    p_v = np.stack([outs[PROMPT_CORE[b]]["p_v"] for b in range(4)])[None].astype(f32)
    p_S = np.stack([outs[PROMPT_CORE[b]]["p_S"] for b in range(4)])[None].astype(f32)
    p_f = np.stack([outs[PROMPT_CORE[b]]["p_f"] for b in range(4)], axis=1).astype(f32)
    s_h = np.concatenate([outs[i]["s_h"] for i in range(N_CORES)], axis=1).astype(f32)
    s_c = np.concatenate([outs[i]["s_c"] for i in range(N_CORES)], axis=1).astype(f32)
    s_k = np.concatenate([outs[i]["s_k"] for i in range(N_CORES)])[None].astype(f32)
    s_v = np.concatenate([outs[i]["s_v"] for i in range(N_CORES)])[None].astype(f32)
    s_S = np.concatenate([outs[i]["s_S"] for i in range(N_CORES)])[None].astype(f32)
    s_f = np.concatenate([outs[i]["s_f"] for i in range(N_CORES)], axis=1).astype(f32)
    return tuple(np.ascontiguousarray(a) for a in
                 (y_prompt, y_sample, p_h, p_c, p_k, p_v, p_S, p_f, s_h, s_c, s_k, s_v, s_S, s_f))
```

```python
import numpy as np
from contextlib import ExitStack
import concourse.bass as bass
import concourse.mybir as mybir
from concourse.bass_utils import run_bass_kernel_spmd

F32 = mybir.dt.float32
BF16 = mybir.dt.bfloat16
AF = mybir.ActivationFunctionType
ALU = mybir.AluOpType

D = 2048
NDC = 16
DR = 2560
NRC = 20
DFF = 6144
NFC = 48
TT = 512
SEQ = 2048
NT = SEQ // TT
NS = 4
EPS = 1e-6
DEPTH = 4
PAST = 16384
MIX = (0, 1, 2, 0)
SLOT = (0, 0, 0, 1)
GELU_C = 1.5957691216057308


class Reg:
    __slots__ = ("name", "lw", "rd")

    def __init__(self, name):
        self.name = name
        self.lw = None
        self.rd = []


class Op:
    __slots__ = ("eng", "fn", "deps", "is_dma", "marked", "idx", "dsem", "dtarget")

    def __init__(self, eng, fn, is_dma):
        self.eng = eng
        self.fn = fn
        self.deps = []
        self.is_dma = is_dma
        self.marked = False
        self.idx = 0
        self.dsem = None
        self.dtarget = 0


ENGS = ("pe", "act", "dve", "pool", "sp")


class Sched:
    N_DMA_SEMS = 32

    def __init__(self, nc, same_engine_sync=True):
        self.nc = nc
        self.ops = []
        self.same_engine_sync = same_engine_sync
        self.dma_rr = 0
        self.dma_count = [0] * self.N_DMA_SEMS
        self.dma_last = [None] * self.N_DMA_SEMS

    def _add(self, op, r, w):
        deps = set()
        for reg in r:
            if reg.lw is not None:
                deps.add(reg.lw)
        for reg in w:
            if reg.lw is not None:
                deps.add(reg.lw)
            for o in reg.rd:
                deps.add(o)
        deps.discard(op)
        op.deps = list(deps)
        for reg in r:
            if not op.is_dma:
                reg.rd = [o for o in reg.rd if o.is_dma or o.eng != op.eng]
            reg.rd.append(op)
        for reg in w:
            reg.lw = op
            reg.rd = []
        self.ops.append(op)
        return op

    def op(self, eng, fn, r=(), w=()):
        return self._add(Op(eng, fn, False), r, w)

    def dma(self, eng, fn, r=(), w=()):
        op = Op(eng, fn, True)
        k = self.dma_rr
        self.dma_rr = (self.dma_rr + 1) % self.N_DMA_SEMS
        op.dsem = k
        self.dma_count[k] += 1
        op.dtarget = 16 * self.dma_count[k]
        prev = self.dma_last[k]
        self.dma_last[k] = op
        self._add(op, r, w)
        if prev is not None:
            op.deps.append(prev)
        return op

    def emit(self, stack):
        nc = self.nc
        sync_same = self.same_engine_sync
        for o in self.ops:
            for d in o.deps:
                if d.is_dma:
                    continue
                if d.eng == o.eng and not o.is_dma and (d.eng == "pe" or not sync_same):
                    continue
                d.marked = True
        cnt = {e: 0 for e in ENGS}
        for o in self.ops:
            if o.marked and not o.is_dma:
                cnt[o.eng] += 1
                o.idx = cnt[o.eng]
        esem = {e: stack.enter_context(nc.semaphore("es_" + e)) for e in ENGS}
        dsem = [stack.enter_context(nc.semaphore("ds_%d" % i)) for i in range(self.N_DMA_SEMS)]
        per = {e: [] for e in ENGS}
        for o in self.ops:
            per[o.eng].append(o)
        final_dma = [(dsem[k], 16 * self.dma_count[k]) for k in range(self.N_DMA_SEMS) if self.dma_count[k]]
        final_eng = [(esem[e], cnt[e]) for e in ENGS if cnt[e]]

        def run(ename, eng):
            waited = {}
            for o in per[ename]:
                need = {}
                for d in o.deps:
                    if d.is_dma:
                        key = ("d", d.dsem)
                        val = d.dtarget
                        sem = dsem[d.dsem]
                    else:
                        if d.eng == ename and not o.is_dma and (ename == "pe" or not sync_same):
                            continue
                        key = ("e", d.eng)
                        val = d.idx
                        sem = esem[d.eng]
                    if val > need.get(key, (None, 0))[1]:
                        need[key] = (sem, val)
                for key, (sem, val) in need.items():
                    if waited.get(key, 0) >= val:
                        continue
                    eng.wait_ge(sem, val)
                    waited[key] = val
                ins = o.fn(eng)
                if o.is_dma:
                    ins.then_inc(dsem[o.dsem], 16)
                elif o.marked:
                    ins.then_inc(esem[ename], 1)
            if ename == "sp":
                for sem, val in final_dma + final_eng:
                    eng.wait_ge(sem, val)

        block = stack.enter_context(nc.Block())

        @block.sync
        def _(e):
            run("sp", e)

        @block.tensor
        def _(e):
            run("pe", e)

        @block.scalar
        def _(e):
            run("act", e)

        @block.vector
        def _(e):
            run("dve", e)

        @block.gpsimd
        def _(e):
            run("pool", e)


def fm(v):
    v = np.asarray(v, np.float32)
    sh = v.shape
    v = v.reshape(sh[:-1] + (sh[-1] // 128, 128))
    return np.ascontiguousarray(np.moveaxis(v, -1, 0))


def band_blocks():
    out = []
    for j in range(NRC):
        b0 = (128 * j) // 160
        b1 = (128 * j + 127) // 160
        k0 = (160 * b0) // 128
        k1 = (160 * (b1 + 1) - 1) // 128
        out.append(list(range(k0, k1 + 1)))
    return out


BAND = band_blocks()
NBAND = sum(len(b) for b in BAND)


def band_layout(w):
    dense = np.zeros((DR, DR), np.float32)
    for n in range(16):
        dense[160 * n:160 * n + 160, 160 * n:160 * n + 160] = w[n]
    res = np.zeros((128, 64 * 128), np.float32)
    bi = 0
    for j in range(NRC):
        for kc in BAND[j]:
            res[:, bi * 128:(bi + 1) * 128] = dense[kc * 128:(kc + 1) * 128, j * 128:(j + 1) * 128]
            bi += 1
    return res


class PP:
    def __init__(self):
        self.cols = []
        self.off = {}
        self.n = 0

    def add(self, name, arr):
        arr = np.asarray(arr, np.float32).reshape(128, -1)
        self.off[name] = (self.n, arr.shape[1])
        self.cols.append(arr)
        self.n += arr.shape[1]

    def array(self):
        return np.ascontiguousarray(np.concatenate(self.cols, axis=1))


def rope_tables(pos):
    pos = np.asarray(pos, np.float32)
    half = 8
    inv = (np.float32(500000.0) ** (-np.arange(half, dtype=np.float32) * np.float32(2.0 / 16))).astype(np.float32)
    ang = pos[None, :] * inv[:, None]
    cos = np.cos(ang).astype(np.float32)
    sin = np.sin(ang).astype(np.float32)
    C = np.ones((128, len(pos)), np.float32)
    Sg = np.zeros((128, len(pos)), np.float32)
    for p in range(128):
        d = p % 64
        if d < 8:
            C[p] = cos[d]
            Sg[p] = -sin[d]
        elif d < 16:
            C[p] = cos[d - 8]
            Sg[p] = sin[d - 8]
    return C, Sg


def qk_cols():
    cols = list(range(2048))
    for g in range(4):
        base = 2048 + 64 * g
        cols += list(range(base, base + 64)) * 2
    cols = np.array(cols)
    rot = cols.copy()
    for i, c in enumerate(cols):
        hd = c % 64
        if hd < 8:
            rot[i] = c + 8
        elif hd < 16:
            rot[i] = c - 8
    return cols, rot


def v_cols():
    cols = []
    for g in range(4):
        base = 2304 + 64 * g
        cols += list(range(base, base + 64)) * 2
    return np.array(cols)


class Buf:
    def __init__(self, B, off, nch, C, dt, name):
        self.dt = dt
        self.nch = nch
        self.C = C
        cw = C if dt == F32 else C // 2
        self.cw = cw
        self.off = off
        self.words = nch * cw
        assert off + self.words <= B.ARENA, (name, off, self.words)
        raw = B.arena[:, off:off + nch * cw]
        if dt != F32:
            raw = raw.bitcast(dt)
        self.ap = raw.rearrange("p (n c) -> p n c", n=nch)
        self.B = B

    def r(self, c0=0, c1=None):
        if c1 is None:
            c1 = c0 + 1
        return self.B.aregs(self.off + c0 * self.cw, self.off + c1 * self.cw)

    def all(self):
        return self.r(0, self.nch)

    @property
    def end(self):
        return self.off + self.words


GELU_F = AF.Gelu_apprx_tanh
SAME_ENGINE_SYNC = True
NCST = 128 + 128 + 512 + 512
NMSK = 512 + 512 + 128


class Builder:
    ARENA = 19456
    ABLK = 256
    NORM_OFF = 18432 - 1536

    def __init__(self):
        nc = self.nc = bass.Bass("TRN2", target_bir_lowering=False)
        self.st = ExitStack()
        self.S = Sched(nc, same_engine_sync=SAME_ENGINE_SYNC)
        self.din = {}
        self.dout = {}
        self._bank_i = 0
        self._slab_i = 0
        self._reserved = set()
        self.slab_cache = {}
        self._slab_pinned = set()
        self._slab_n = 0
        self._deferred = set()
        self._defer_frac = 3
        self._last_slab = 0
        self.use_cache = True

    def inp(self, name, shape):
        t = self.nc.dram_tensor(name, list(shape), F32, kind="ExternalInput").ap()
        self.din[name] = tuple(shape)
        return t

    def outp(self, name, shape):
        t = self.nc.dram_tensor(name, list(shape), F32, kind="ExternalOutput").ap()
        self.dout[name] = tuple(shape)
        return t

    def sb(self, name, shape, dt=F32):
        return self.st.enter_context(self.nc.sbuf_tensor(name, list(shape), dt))

    def aregs(self, w0, w1):
        return self.areg[w0 // self.ABLK:(w1 + self.ABLK - 1) // self.ABLK]

    def bank(self):
        while self._bank_i in self._reserved:
            self._bank_i = (self._bank_i + 1) % 8
        i = self._bank_i
        self._bank_i = (i + 1) % 8
        return self.ps[i], self.rps[i]

    def op(self, eng, fn, r=(), w=()):
        return self.S.op(eng, fn, r, w)

    def dma(self, fn, r=(), w=(), eng="sp"):
        return self.S.dma(eng, fn, r, w)

    def next_slab(self):
        while self._slab_i in self._slab_pinned:
            self._slab_i = (self._slab_i + 1) % 3
        i = self._slab_i
        self._slab_i = (i + 1) % 3
        self._last_slab = i
        return self.slabs[i], self.rslab[i]

    def load_slab(self, W2d, k0, nk, c0, ncol):
        t, regs = self.next_slab()
        key = (W2d.tensor.name, k0, nk, c0, ncol)
        hit = self.slab_cache.get(key)
        if hit is not None:
            scr = hit
            for a in range(0, nk, 8):
                b = min(nk, a + 8)
                h = a // 8
                self.S.dma("sp", lambda e, t=t, scr=scr, a=a, b=b, ncol=ncol: e.dma_start(out=t[:, a:b, 0:ncol], in_=scr[:, a:b, 0:ncol]),
                           w=[regs[2 * h], regs[2 * h + 1]])
            return t, (lambda kc: [regs[kc // 4]])
        src = W2d[k0 * 128:(k0 + nk) * 128, c0:c0 + ncol].rearrange("(kc p) n -> p kc n", p=128)
        for a in range(0, nk, 8):
            b = min(nk, a + 8)
            h = a // 8
            self.S.dma("pool", lambda e, t=t, src=src, a=a, b=b, ncol=ncol: e.dma_start(out=t[:, a:b, 0:ncol], in_=src[:, a:b, :]),
                       w=[regs[2 * h], regs[2 * h + 1]])
        self._slab_n += 1
        if self.use_cache and not (self._defer_frac and key not in self._deferred and (self._slab_n % 10) < self._defer_frac
                                   and not self._deferred.add(key)):
            scr = self.nc.dram_tensor("wscr%d" % len(self.slab_cache), [128, nk, ncol], BF16).ap()
            self.slab_cache[key] = scr
            for a in range(0, nk, 8):
                b = min(nk, a + 8)
                h = a // 8
                self.S.dma("sp", lambda e, t=t, scr=scr, a=a, b=b, ncol=ncol: e.dma_start(out=scr[:, a:b, 0:ncol], in_=t[:, a:b, 0:ncol]),
                           r=[regs[2 * h], regs[2 * h + 1]])
        return t, (lambda kc: [regs[kc // 4]])

    def load_band(self, Wb):
        t, regs = self.next_slab()
        flat = t[:].rearrange("p a b -> p (a b)")
        key = (Wb.tensor.name, "band")
        hit = self.slab_cache.get(key)
        if hit is not None:
            for q in range(4):
                self.S.dma("sp", lambda e, q=q, hit=hit: e.dma_start(out=flat[:, q * 2048:(q + 1) * 2048], in_=hit[:, q * 2048:(q + 1) * 2048]),
                           w=[regs[q]])
            return flat, (lambda b: [regs[b // 16]])
        for q in range(4):
            self.S.dma("pool", lambda e, q=q: e.dma_start(out=flat[:, q * 2048:(q + 1) * 2048], in_=Wb[:, q * 2048:(q + 1) * 2048]),
                       w=[regs[q]])
        if self.use_cache:
            scr = self.nc.dram_tensor("wscr%d" % len(self.slab_cache), [128, 8192], BF16).ap()
            self.slab_cache[key] = scr
            for q in range(4):
                self.S.dma("sp", lambda e, q=q, scr=scr: e.dma_start(out=scr[:, q * 2048:(q + 1) * 2048], in_=flat[:, q * 2048:(q + 1) * 2048]),
                           r=[regs[q]])
        return flat, (lambda b: [regs[b // 16]])

    def proj_fm(self, W2d, nkc, inp, rin, C, c_start, nout, evac, kslab=16):
        for g0 in range(0, nout, 4):
            ng = min(4, nout - g0)
            banks = [self.bank() for _ in range(ng)]
            nks = (nkc + kslab - 1) // kslab
            for ks in range(nks):
                k0 = ks * kslab
                nk = min(kslab, nkc - k0)
                slab, rs = self.load_slab(W2d, k0, nk, c_start + g0 * 128, ng * 128)
                for jj in range(ng):
                    pst, prg = banks[jj]
                    for kc in range(nk):
                        self.op("pe", lambda e, pst=pst, slab=slab, kc=kc, jj=jj, k0=k0, first=(ks == 0 and kc == 0),
                                last=(ks == nks - 1 and kc == nk - 1): e.matmul(
                            pst[:, 0:C], lhsT=slab[:, kc, jj * 128:(jj + 1) * 128], rhs=inp[:, k0 + kc, 0:C], start=first, stop=last),
                            r=rs(kc) + rin(k0 + kc), w=[prg])
            idx = [self.ps.index(bk[0]) for bk in banks]
            self._reserved.update(idx)
            for jj in range(ng):
                self._reserved.discard(idx[jj])
                evac(g0 + jj, banks[jj][0], banks[jj][1])

    def setup(self, pp_off, npp):
        nc = self.nc
        self.pp_off = pp_off
        self.x = self.sb("x", [128, NDC, TT])
        self.rx = [Reg("x%d" % c) for c in range(NDC)]
        self.xn = self.sb("xn", [128, NDC, TT], BF16)
        self.rxn = [Reg("xn%d" % c) for c in range(NDC)]
        self.slabs = [self.sb("slab%d" % i, [128, 16, 512], BF16) for i in range(3)]
        self.rslab = [[Reg("slab%d_%d" % (i, q)) for q in range(4)] for i in range(3)]
        self.ps = [self.st.enter_context(nc.psum_tensor("ps%d" % i, [128, 512], F32)) for i in range(8)]
        self.rps = [Reg("ps%d" % i) for i in range(8)]
        self.arena = self.sb("arena", [128, self.ARENA])
        self.areg = [Reg("ar%d" % i) for i in range(self.ARENA // self.ABLK)]
        self.pp = self.sb("pp_sb", [128, npp])
        self.rpp = Reg("pp")
        self.ppd = self.inp("pp", [128, npp])
        self.dma(lambda e: e.dma_start(out=self.pp[:], in_=self.ppd), w=[self.rpp])
        self.cst_d = self.inp("cst", [128, NCST])
        self.cst = self.sb("cstt", [128, NCST])
        self.rcst = Reg("cst")
        self.dma(lambda e: e.dma_start(out=self.cst[:], in_=self.cst_d), w=[self.rcst])
        self.ident = self.cst[:, 0:128]
        self.ones = self.cst[:, 128:256]
        self.m01 = self.cst[:, 256:768]
        self.sel = self.cst[:, 768:1280]
        self.msk_d = self.inp("msk", [128, NMSK])
        mtmp = self.arena[:, 0:NMSK]
        mr = self.aregs(0, NMSK)
        self.dma(lambda e: e.dma_start(out=mtmp, in_=self.msk_d), w=mr)
        self.cstb = self.sb("cstb", [128, NMSK + 128], BF16)
        self.rcstb = Reg("cstb")
        self.op("dve", lambda e: e.tensor_copy(out=self.cstb[:, 0:NMSK], in_=mtmp), r=mr, w=[self.rcstb])
        self.op("dve", lambda e: e.tensor_copy(out=self.cstb[:, NMSK:NMSK + 128], in_=self.cst[:, 128:256]), r=[self.rcst], w=[self.rcstb])
        self.mprev4 = self.cstb[:, 0:512]
        self.mcur4 = self.cstb[:, 512:1024]
        self.mhg = self.cstb[:, 1024:1152]
        self.onesb = self.cstb[:, 1152:1280]
        self.lru_cb = self.sb("lru_cb", [128, 2, NRC, 3])
        self.lru_h = self.sb("lru_hc", [128, 2, NRC])
        self.ffn_cb = self.sb("ffn_cb", [128, DEPTH, NFC, 2])
        self.r_lcb = [Reg("lcb%d" % i) for i in range(2)]
        self.r_lh = [Reg("lh%d" % i) for i in range(2)]
        self.r_fcb = [Reg("fcb%d" % i) for i in range(DEPTH)]
        self.Sst = self.sb("Sst", [128, 16, 128])
        self.rS = [Reg("S%d" % i) for i in range(16)]
        self.kcar = self.sb("kcar", [128, 4, 128], BF16)
        self.vcar = self.sb("vcar", [128, 512], BF16)
        self.r_kcar = Reg("kcar")
        self.r_vcar = Reg("vcar")
        self.s_h = self.sb("s_h_sb", [128, 2, NRC, NS])
        self.s_c = self.sb("s_c_sb", [128, 2, NRC, 3, NS])
        self.s_f = self.sb("s_f_sb", [128, DEPTH, NFC, 2, NS])
        self.r_sh = [Reg("sh%d" % i) for i in range(2)]
        self.r_sc = [Reg("sc%d" % i) for i in range(2)]
        self.r_sf = [Reg("sf%d" % i) for i in range(DEPTH)]
        no = self.NORM_OFF
        self.rstd = self.arena[:, no:no + 512]
        self.r_rstd = self.aregs(no, no + 512)
        self.sq = [self.arena[:, no + 512 * (i + 1):no + 512 * (i + 2)] for i in range(2)]
        self.r_sq = [self.aregs(no + 512 * (i + 1), no + 512 * (i + 2)) for i in range(2)]
        self.c1 = self.sb("c1", [128, 2, NRC])
        self.r_c1 = Reg("c1")
        self.hlb = self.sb("hlb", [128, 2, NDC])
        self.r_hlb = Reg("hlb")
        self.esk = self.sb("esk", [128, 32])
        self.r_esk = Reg("esk")
        self.eps_t = self.sb("eps_t", [128, 1])
        self.r_eps = Reg("eps")
        self.op("dve", lambda e: e.memset(self.eps_t[:], EPS), w=[self.r_eps])
        self.one_t = self.sb("one_t", [128, 1])
        self.op("dve", lambda e: e.memset(self.one_t[:], 1.0), w=[self.r_eps])
        self.op("dve", lambda e: e.memset(self.lru_cb[:], 0.0), w=self.r_lcb)
        self.op("dve", lambda e: e.memset(self.lru_h[:], 0.0), w=self.r_lh)
        self.op("dve", lambda e: e.memset(self.ffn_cb[:], 0.0), w=self.r_fcb)
        self.op("dve", lambda e: e.memset(self.Sst[:], 0.0), w=self.rS)

    def P(self, name, c=None, n=1):
        off, ln = self.pp_off[name]
        if c is None:
            return self.pp[:, off:off + ln]
        return self.pp[:, off + c:off + c + n]

    def precompute(self):
        for j in range(2):
            self.op("act", lambda e, j=j: e.activation(out=self.c1[:, j, :], in_=self.P("lru%d_lam" % j), func=AF.Exp, scale=-1.0),
                    r=[self.rpp], w=[self.r_c1])
            self.op("act", lambda e, j=j: e.activation(out=self.c1[:, j, :], in_=self.c1[:, j, :], func=AF.Ln, bias=self.one_t[:]),
                    r=[self.r_c1, self.r_eps], w=[self.r_c1])
            self.op("dve", lambda e, j=j: e.tensor_scalar(out=self.c1[:, j, :], in0=self.c1[:, j, :], scalar1=-8.0, scalar2=None, op0=ALU.mult),
                    r=[self.r_c1], w=[self.r_c1])
        o0 = 2048
        t = self.arena[:, o0:o0 + 64].rearrange("p (l c) -> p l c", l=4)
        den = self.arena[:, o0 + 64:o0 + 80]
        num = self.arena[:, o0 + 80:o0 + 96]
        tr = self.aregs(o0, o0 + 96)
        self.op("act", lambda e: e.activation(out=t, in_=self.P("hg_lbl").rearrange("p (l c) -> p l c", l=4), func=AF.Exp), r=[self.rpp], w=tr)
        self.op("dve", lambda e: e.tensor_tensor(out=num, in0=t[:, 1, :], in1=t[:, 2, :], op=ALU.add), r=tr, w=tr)
        self.op("dve", lambda e: e.tensor_tensor(out=den, in0=t[:, 0, :], in1=t[:, 3, :], op=ALU.add), r=tr, w=tr)
        self.op("dve", lambda e: e.tensor_tensor(out=den, in0=den, in1=num, op=ALU.add), r=tr, w=tr)
        self.op("dve", lambda e: e.reciprocal(out=den, in_=den), r=tr, w=tr)
        self.op("dve", lambda e: e.tensor_tensor(out=self.hlb[:, 0, :], in0=num, in1=den, op=ALU.mult), r=tr, w=[self.r_hlb])
        self.op("dve", lambda e: e.tensor_scalar(out=self.hlb[:, 1, :], in0=self.hlb[:, 0, :], scalar1=-1.0, scalar2=1.0, op0=ALU.mult, op1=ALU.add),
                r=[self.r_hlb], w=[self.r_hlb])
        self.op("act", lambda e: e.activation(out=self.esk[:], in_=self.P("swa_sinks"), func=AF.Exp), r=[self.rpp], w=[self.r_esk])

    def rmsnorm(self, gname, C, out=None, rout=None):
        pst, prg = self.bank()
        for c in range(NDC):
            sq, rsq = self.sq[c % 2].bitcast(BF16), self.r_sq[c % 2]
            self.op("act", lambda e, c=c, sq=sq: e.activation(out=sq[:, 0:C], in_=self.x[:, c, 0:C], func=AF.Square), r=[self.rx[c]], w=rsq)
            self.op("pe", lambda e, c=c, sq=sq: e.matmul(pst[:, 0:C], lhsT=self.onesb, rhs=sq[:, 0:C], start=(c == 0), stop=(c == NDC - 1)),
                    r=rsq + [self.rcstb], w=[prg])
        self.op("act", lambda e: e.activation(out=self.rstd[:, 0:C], in_=pst[:, 0:C], func=AF.Ln, scale=1.0 / D, bias=self.eps_t[:]),
                r=[prg, self.r_eps], w=self.r_rstd)
        self.op("act", lambda e: e.activation(out=self.rstd[:, 0:C], in_=self.rstd[:, 0:C], func=AF.Exp, scale=-0.5), r=self.r_rstd, w=self.r_rstd)
        for c in range(NDC):
            if out is None:
                o_ap, o_r = self.xn[:, c, 0:C], [self.rxn[c]]
            else:
                o_ap, o_r = out(c), rout(c)
            self.op("dve", lambda e, c=c, o_ap=o_ap: e.scalar_tensor_tensor(out=o_ap, in0=self.x[:, c, 0:C], scalar=self.P(gname, c),
                                                                           in1=self.rstd[:, 0:C], op0=ALU.mult, op1=ALU.mult),
                    r=[self.rx[c], self.rpp] + self.r_rstd, w=o_r)

    def rxn_f(self, kc):
        return [self.rxn[kc]]

    def resid_add(self, j, pst, prg, C, bias=None):
        if bias is None:
            self.op("dve", lambda e: e.tensor_tensor(out=self.x[:, j, 0:C], in0=pst[:, 0:C], in1=self.x[:, j, 0:C], op=ALU.add),
                    r=[prg, self.rx[j]], w=[self.rx[j]])
        else:
            self.op("dve", lambda e: e.scalar_tensor_tensor(out=self.x[:, j, 0:C], in0=pst[:, 0:C], scalar=bias, in1=self.x[:, j, 0:C],
                                                           op0=ALU.add, op1=ALU.add),
                    r=[prg, self.rx[j], self.rpp], w=[self.rx[j]])

    def conv_psum(self, mode, pst, prg, taps, bias, cb, rcb, out, rout, C, stt=None, rst=None):
        W = len(taps)
        self.op("dve", lambda e: e.tensor_scalar(out=out, in0=pst[:, 0:C], scalar1=taps[W - 1], scalar2=bias, op0=ALU.mult, op1=ALU.add),
                r=[prg, self.rpp], w=rout)
        if mode == "p":
            for s in range(1, W):
                tp = taps[W - 1 - s]
                self.op("dve", lambda e, s=s, tp=tp: e.scalar_tensor_tensor(out=out[:, s:C], in0=pst[:, 0:C - s], scalar=tp, in1=out[:, s:C],
                                                                             op0=ALU.mult, op1=ALU.add), r=[prg, self.rpp] + rout, w=rout)
                self.op("dve", lambda e, s=s, tp=tp: e.scalar_tensor_tensor(out=out[:, 0:s], in0=cb[:, W - 1 - s:W - 1], scalar=tp, in1=out[:, 0:s],
                                                                             op0=ALU.mult, op1=ALU.add), r=[rcb, self.rpp] + rout, w=rout)
            self.op("act", lambda e: e.copy(out=cb, in_=pst[:, C - (W - 1):C]), r=[prg], w=[rcb])
        else:
            for j in range(W - 1):
                self.op("dve", lambda e, j=j: e.scalar_tensor_tensor(out=out, in0=stt[:, j, :], scalar=taps[j], in1=out, op0=ALU.mult, op1=ALU.add),
                        r=[rst, self.rpp] + rout, w=rout)
            for j in range(W - 2):
                self.op("act", lambda e, j=j: e.copy(out=stt[:, j, :], in_=stt[:, j + 1, :]), r=[rst], w=[rst])
            self.op("act", lambda e: e.copy(out=stt[:, W - 2, :], in_=pst[:, 0:C]), r=[prg], w=[rst])

    def ffn(self, layer, mode, C):
        W_in, W_out = self.w_ffn_in[layer], self.w_ffn_out[layer]
        self.rmsnorm("nffn%d" % layer, C)
        h = Buf(self, 0, NFC, TT, BF16, "ffn_h")
        gact = Buf(self, h.end, 4, TT, BF16, "ffn_gact")
        xc = Buf(self, gact.end, 2, TT, F32, "ffn_xc")
        for g in range(NFC // 4):
            def evac_gate(j, pst, prg, g=g):
                c = g * 4 + j
                xo, xr = xc.ap[:, c % 2, 0:C], xc.r(c % 2)
                taps = [self.P("ffn%d_cw" % layer, k * NFC + c) for k in range(3)]
                self.conv_psum(mode, pst, prg, taps, self.P("ffn%d_cb" % layer, c), self.ffn_cb[:, layer, c, :], self.r_fcb[layer], xo, xr, C,
                               stt=self.s_f[:, layer, c, :, :], rst=self.r_sf[layer])
                self.op("act", lambda e: e.activation(out=gact.ap[:, j, 0:C], in_=xo, func=GELU_F), r=xr, w=gact.r(j))

            self.proj_fm(W_in, NDC, self.xn, self.rxn_f, C, g * 512, 4, evac_gate)

            def evac_up(j, pst, prg, g=g):
                c = g * 4 + j
                self.op("dve", lambda e: e.tensor_tensor(out=h.ap[:, c, 0:C], in0=pst[:, 0:C], in1=gact.ap[:, j, 0:C], op=ALU.mult),
                        r=[prg] + gact.r(j), w=h.r(c))

            self.proj_fm(W_in, NDC, self.xn, self.rxn_f, C, DFF + g * 512, 4, evac_up)
        self.proj_fm(W_out, NFC, h.ap, lambda kc: h.r(kc), C, 0, NDC, lambda j, pst, prg: self.resid_add(j, pst, prg, C))

    def lru(self, layer, mode, C):
        sl = SLOT[layer]
        W_in, W_out, Wa, Wi = self.w_lru_in[sl], self.w_lru_out[sl], self.w_lru_a[sl], self.w_lru_i[sl]
        nm = "lru%d_" % sl
        self.rmsnorm("nmix%d" % layer, C)
        xc = Buf(self, 0, NRC, TT, F32, "lru_xc")
        gg = Buf(self, xc.end, NRC, TT, BF16, "lru_gg")
        xcb = Buf(self, gg.end, 4, TT, BF16, "lru_xcb")
        tm = Buf(self, xcb.end, 6, TT, F32, "lru_tmp")

        def evac_gate(j, pst, prg):
            self.op("act", lambda e: e.activation(out=gg.ap[:, j, 0:C], in_=pst[:, 0:C], func=GELU_F), r=[prg], w=gg.r(j))

        def evac_xr(c, pst, prg):
            taps = [self.P(nm + "cw", k * NRC + c) for k in range(4)]
            self.conv_psum(mode, pst, prg, taps, self.P(nm + "cb", c), self.lru_cb[:, sl, c, :], self.r_lcb[sl], xc.ap[:, c, 0:C], xc.r(c), C,
                           stt=self.s_c[:, sl, c, :, :], rst=self.r_sc[sl])

        self.proj_fm(W_in, NDC, self.xn, self.rxn_f, C, DR, NRC, evac_xr)
        sa, rsa = self.load_band(Wa)
        pin_a = self._last_slab
        si, rsi = self.load_band(Wi)
        pin_i = self._last_slab
        self._slab_pinned.update((pin_a, pin_i))
        cast_done = set()
        bi = 0
        for j0 in range(0, NRC, 2):
            if j0 % 4 == 0:
                g = j0 // 4
                self.proj_fm(W_in, NDC, self.xn, self.rxn_f, C, g * 512, 4, lambda j, pst, prg, g=g: evac_gate(g * 4 + j, pst, prg))
            pend = []
            for j in (j0, j0 + 1):
                for kc in BAND[j]:
                    if kc not in cast_done:
                        cast_done.add(kc)
                        self.op("dve", lambda e, kc=kc: e.tensor_copy(out=xcb.ap[:, kc % 4, 0:C], in_=xc.ap[:, kc, 0:C]), r=xc.r(kc), w=xcb.r(kc % 4))
                pa, rpa = self.bank()
                pi, rpi = self.bank()
                nb = len(BAND[j])
                for (slab, rs, pst, prg) in ((sa, rsa, pa, rpa), (si, rsi, pi, rpi)):
                    for n, kc in enumerate(BAND[j]):
                        b_ = bi + n
                        self.op("pe", lambda e, slab=slab, pst=pst, kc=kc, b=bi + n, n=n, nb=nb: e.matmul(
                            pst[:, 0:C], lhsT=slab[:, b * 128:(b + 1) * 128], rhs=xcb.ap[:, kc % 4, 0:C], start=(n == 0), stop=(n == nb - 1)),
                            r=rs(b_) + xcb.r(kc % 4), w=[prg])
                bi += nb
                pend.append((j, pa, rpa, pi, rpi))
            self._lru_tail2(pend, mode, C, sl, nm, tm, xc, gg)
        self._slab_pinned.difference_update((pin_a, pin_i))
        self.proj_fm(W_out, NRC, gg.ap, lambda kc: gg.r(kc), C, 0, NDC, lambda j, pst, prg: self.resid_add(j, pst, prg, C))

    def _lru_tail2(self, pend, mode, C, sl, nm, tm, xc, gg):
        T3 = []
        for k, (j, pa, rpa, pi, rpi) in enumerate(pend):
            o3 = 3 * k
            T3.append(([tm.ap[:, o3 + q, 0:C] for q in range(3)], [tm.r(o3 + q) for q in range(3)]))
        for k, (j, pa, rpa, pi, rpi) in enumerate(pend):
            (ta, ti_, tm_), (ra, ri, rm) = T3[k]
            self.op("act", lambda e, j=j, pa=pa, ta=ta: e.activation(out=ta, in_=pa[:, 0:C], func=AF.Sigmoid, bias=self.P(nm + "ba", j)),
                    r=[rpa, self.rpp], w=ra)
            self.op("act", lambda e, j=j, pi=pi, ti_=ti_: e.activation(out=ti_, in_=pi[:, 0:C], func=AF.Sigmoid, bias=self.P(nm + "bi", j)),
                    r=[rpi, self.rpp], w=ri)
        for k, (j, pa, rpa, pi, rpi) in enumerate(pend):
            (ta, ti_, tm_), (ra, ri, rm) = T3[k]
            self.op("act", lambda e, j=j, ta=ta: e.activation(out=ta, in_=ta, func=AF.Exp, scale=self.c1[:, sl, j:j + 1]), r=ra + [self.r_c1], w=ra)
            self.op("dve", lambda e, ta=ta, tm_=tm_: e.tensor_tensor(out=tm_, in0=ta, in1=ta, op=ALU.mult), r=ra, w=rm)
            self.op("dve", lambda e, j=j, ti_=ti_: e.tensor_tensor(out=ti_, in0=ti_, in1=xc.ap[:, j, 0:C], op=ALU.mult), r=ri + xc.r(j), w=ri)
        for k, (j, pa, rpa, pi, rpi) in enumerate(pend):
            (ta, ti_, tm_), (ra, ri, rm) = T3[k]
            self.op("act", lambda e, tm_=tm_: e.activation(out=tm_, in_=tm_, func=AF.Sqrt, scale=-1.0, bias=self.one_t[:]), r=rm + [self.r_eps], w=rm)
        for k, (j, pa, rpa, pi, rpi) in enumerate(pend):
            (ta, ti_, tm_), (ra, ri, rm) = T3[k]
            th, rh = tm_, rm
            self.op("dve", lambda e, ti_=ti_, tm_=tm_: e.tensor_tensor(out=ti_, in0=ti_, in1=tm_, op=ALU.mult), r=ri + rm, w=ri)
            if mode == "p":
                self.op("dve", lambda e, j=j, ta=ta, ti_=ti_, th=th: e.tensor_tensor_scan(out=th, data0=ta, data1=ti_, initial=self.lru_h[:, sl, j:j + 1],
                                                                                      op0=ALU.mult, op1=ALU.add), r=ra + ri + [self.r_lh[sl]], w=rh)
                self.op("dve", lambda e, j=j, th=th: e.tensor_copy(out=self.lru_h[:, sl, j:j + 1], in_=th[:, C - 1:C]), r=rh, w=[self.r_lh[sl]])
            else:
                self.op("dve", lambda e, j=j, ta=ta, th=th: e.tensor_tensor(out=th, in0=ta, in1=self.s_h[:, sl, j, :], op=ALU.mult),
                        r=ra + [self.r_sh[sl]], w=rh)
                self.op("dve", lambda e, th=th, ti_=ti_: e.tensor_tensor(out=th, in0=th, in1=ti_, op=ALU.add), r=rh + ri, w=rh)
                self.op("dve", lambda e, j=j, th=th: e.tensor_copy(out=self.s_h[:, sl, j, :], in_=th), r=rh, w=[self.r_sh[sl]])
            self.op("dve", lambda e, j=j, th=th: e.tensor_tensor(out=gg.ap[:, j, 0:C], in0=gg.ap[:, j, 0:C], in1=th, op=ALU.mult),
                    r=gg.r(j) + rh, w=gg.r(j))

    def _lru_tail(self, j, pa, rpa, pi, rpi, mode, C, sl, nm, tm, xc, gg):
        o3 = 3 * (j % 2)
        ta, ti_, tm_ = (tm.ap[:, o3 + k, 0:C] for k in range(3))
        ra, ri, rm = (tm.r(o3 + k) for k in range(3))
        th, rh = tm_, rm
        self.op("act", lambda e, j=j, pa=pa: e.activation(out=ta, in_=pa[:, 0:C], func=AF.Sigmoid, bias=self.P(nm + "ba", j)), r=[rpa, self.rpp], w=ra)
        self.op("act", lambda e, j=j, pi=pi: e.activation(out=ti_, in_=pi[:, 0:C], func=AF.Sigmoid, bias=self.P(nm + "bi", j)), r=[rpi, self.rpp], w=ri)
        self.op("act", lambda e, j=j: e.activation(out=ta, in_=ta, func=AF.Exp, scale=self.c1[:, sl, j:j + 1]), r=ra + [self.r_c1], w=ra)
        self.op("act", lambda e: e.activation(out=tm_, in_=ta, func=AF.Square), r=ra, w=rm)
        self.op("act", lambda e: e.activation(out=tm_, in_=tm_, func=AF.Sqrt, scale=-1.0, bias=self.one_t[:]), r=rm + [self.r_eps], w=rm)
        self.op("dve", lambda e, j=j: e.tensor_tensor(out=ti_, in0=ti_, in1=xc.ap[:, j, 0:C], op=ALU.mult), r=ri + xc.r(j), w=ri)
        self.op("dve", lambda e: e.tensor_tensor(out=ti_, in0=ti_, in1=tm_, op=ALU.mult), r=ri + rm, w=ri)
        if mode == "p":
            self.op("dve", lambda e, j=j: e.tensor_tensor_scan(out=th, data0=ta, data1=ti_, initial=self.lru_h[:, sl, j:j + 1],
                                                               op0=ALU.mult, op1=ALU.add), r=ra + ri + [self.r_lh[sl]], w=rh)
            self.op("act", lambda e, j=j: e.copy(out=self.lru_h[:, sl, j:j + 1], in_=th[:, C - 1:C]), r=rh, w=[self.r_lh[sl]])
        else:
            self.op("dve", lambda e, j=j: e.tensor_tensor(out=th, in0=ta, in1=self.s_h[:, sl, j, :], op=ALU.mult), r=ra + [self.r_sh[sl]], w=rh)
            self.op("dve", lambda e: e.tensor_tensor(out=th, in0=th, in1=ti_, op=ALU.add), r=rh + ri, w=rh)
            self.op("act", lambda e, j=j: e.copy(out=self.s_h[:, sl, j, :], in_=th), r=rh, w=[self.r_sh[sl]])
        self.op("dve", lambda e, j=j: e.tensor_tensor(out=gg.ap[:, j, 0:C], in0=gg.ap[:, j, 0:C], in1=th, op=ALU.mult), r=gg.r(j) + rh, w=gg.r(j))

    def swa(self, layer, mode, C, ti=0, last=False):
        self.rmsnorm("nmix%d" % layer, C)
        qT = Buf(self, 0, 16, TT, BF16, "qT")
        oT = Buf(self, qT.end, 16, TT, BF16, "oT")
        kT = Buf(self, oT.end, 4, TT, BF16, "kT")
        kfp = Buf(self, kT.end, 4, TT, F32, "kfp")
        vd = Buf(self, kfp.end, 4, 512, BF16, "vd")
        vfp = Buf(self, vd.end, 1, 512, F32, "vfp")
        tabs = Buf(self, vfp.end, 2, TT, F32, "tabs")
        bvb = Buf(self, tabs.end, 1, 512, F32, "bvb")
        t2 = Buf(self, bvb.end, 1, TT, F32, "t2")
        pT = Buf(self, t2.end, 2, 512, BF16, "pT")
        rec = Buf(self, pT.end, 1, 512, F32, "rec")
        aux = Buf(self, rec.end, 2, 512, F32, "aux")
        if mode == "p":
            src = self.rope_d[:, :, ti * TT:(ti + 1) * TT]
        else:
            src = self.rope_d[:, :, SEQ:SEQ + NS]
        self.dma(lambda e: e.dma_start(out=tabs.ap[:, :, 0:C], in_=src), w=tabs.all())
        self.dma(lambda e: e.dma_start(out=bvb.ap[:, 0, :], in_=self.bvb_d), w=bvb.all())
        Ct, St = tabs.ap[:, 0, 0:C], tabs.ap[:, 1, 0:C]

        for g0 in range(0, 20, 4):
            def evac_a(j, pst, prg, g0=g0):
                c = g0 + j
                self.op("dve", lambda e: e.scalar_tensor_tensor(out=kfp.ap[:, j, 0:C], in0=pst[:, 0:C], scalar=self.P("swa_bqk", c), in1=Ct,
                                                               op0=ALU.add, op1=ALU.mult), r=[prg, self.rpp] + tabs.all(), w=kfp.r(j))

            self.proj_fm(self.w_qk, NDC, self.xn, self.rxn_f, C, g0 * 128, 4, evac_a)

            def evac_b(j, pst, prg, g0=g0):
                c = g0 + j
                self.op("dve", lambda e: e.scalar_tensor_tensor(out=t2.ap[:, 0, 0:C], in0=pst[:, 0:C], scalar=self.P("swa_bqkr", c), in1=St,
                                                               op0=ALU.add, op1=ALU.mult), r=[prg, self.rpp] + tabs.all(), w=t2.all())
                if c < 16:
                    self.op("dve", lambda e: e.tensor_tensor(out=qT.ap[:, c, 0:C], in0=kfp.ap[:, j, 0:C], in1=t2.ap[:, 0, 0:C], op=ALU.add),
                            r=kfp.r(j) + t2.all(), w=qT.r(c))
                else:
                    g = c - 16
                    self.op("dve", lambda e: e.tensor_tensor(out=kfp.ap[:, j, 0:C], in0=kfp.ap[:, j, 0:C], in1=t2.ap[:, 0, 0:C], op=ALU.add),
                            r=kfp.r(j) + t2.all(), w=kfp.r(j))
                    self.op("act", lambda e: e.copy(out=kT.ap[:, g, 0:C], in_=kfp.ap[:, j, 0:C]), r=kfp.r(j), w=kT.r(g))
                    if mode == "p" and last:
                        self.dma(lambda e: e.dma_start(out=self.o_pk[:, g, :], in_=kfp.ap[:, j, C - 128:C]), r=kfp.r(j))
                    if mode == "s":
                        self.dma(lambda e: e.dma_start(out=self.o_sknew[:, g, :], in_=kfp.ap[:, j, 0:C]), r=kfp.r(j))

            self.proj_fm(self.w_qkr, NDC, self.xn, self.rxn_f, C, g0 * 128, 4, evac_b)

        slab, rs = self.load_slab(self.w_vd, 0, NDC, 0, 512)
        vcur = []
        if mode == "p":
            for blk in range(4):
                pst, prg = self.bank()
                for kc in range(NDC):
                    self.op("pe", lambda e, kc=kc, blk=blk, pst=pst: e.matmul(pst[:, :], lhsT=self.xn[:, kc, blk * 128:(blk + 1) * 128],
                                                                             rhs=slab[:, kc, :], start=(kc == 0), stop=(kc == NDC - 1)),
                            r=rs(kc) + [self.rxn[kc]], w=[prg])
                self.op("dve", lambda e, pst=pst: e.tensor_tensor(out=vfp.ap[:, 0, :], in0=pst[:, :], in1=bvb.ap[:, 0, :], op=ALU.add),
                        r=[prg] + bvb.all(), w=vfp.all())
                self.op("act", lambda e, blk=blk: e.copy(out=vd.ap[:, blk, :], in_=vfp.ap[:, 0, :]), r=vfp.all(), w=vd.r(blk))
                if last and blk == 3:
                    self.dma(lambda e: e.dma_start(out=self.o_pv, in_=vfp.ap[:, 0, :]), r=vfp.all())
        else:
            for s in range(NS):
                pst, prg = self.bank()
                for kc in range(NDC):
                    self.op("pe", lambda e, kc=kc, s=s, pst=pst: e.matmul(pst[0:1, :], lhsT=self.xn[:, kc, s:s + 1], rhs=slab[:, kc, :],
                                                                         start=(kc == 0), stop=(kc == NDC - 1)), r=rs(kc) + [self.rxn[kc]], w=[prg])
                self.op("dve", lambda e, pst=pst: e.tensor_tensor(out=vfp.ap[0:1, 0, :], in0=pst[0:1, :], in1=bvb.ap[0:1, 0, :], op=ALU.add),
                        r=[prg] + bvb.all(), w=vfp.all())
                self.op("act", lambda e, s=s: e.copy(out=vd.ap[0:1, s, :], in_=vfp.ap[0:1, 0, :]), r=vfp.all(), w=vd.r(s))
                self.dma(lambda e, s=s: e.dma_start(out=self.o_svnew[s:s + 1, :], in_=vfp.ap[0:1, 0, :]), r=vfp.all())

        def attend(kp, vp, kc_, vc_, qf, outf, nq, nkc, sample, rd, wr):
            def one(g, par):
                b = 64 * par
                rhs = qf(g, b)
                N = 4 * nq
                blocks = []
                if kp is not None:
                    blocks.append((kp(g, b), vp(g), 128, 0))
                blocks.append((kc_(g, b), vc_(g), nkc, 1))
                for (kap, vap, nk, which) in blocks:
                    pss, rss = self.bank()
                    self.op("pe", lambda e, pss=pss, kap=kap, nk=nk: e.matmul(pss[0:nk, 0:N], lhsT=kap, rhs=rhs, start=True, stop=True),
                            r=rd, w=[rss])
                    self.op("act", lambda e, pss=pss, nk=nk, which=which: e.activation(out=pT.ap[0:nk, which, 0:N], in_=pss[0:nk, 0:N],
                                                                                       func=AF.Exp, scale=0.125), r=[rss], w=pT.r(which))
                    if not sample:
                        msk = self.mprev4 if which == 0 else self.mcur4
                        self.op("dve", lambda e, which=which, msk=msk: e.tensor_tensor(out=pT.ap[:, which, :], in0=pT.ap[:, which, :], in1=msk,
                                                                                       op=ALU.mult), r=pT.r(which) + [self.rcstb], w=pT.r(which))
                    elif which == 0:
                        self.op("dve", lambda e: e.tensor_scalar(out=pT.ap[:, 0, 0:N], in0=pT.ap[:, 0, 0:N], scalar1=self.mprev4[:, 0:1],
                                                                 scalar2=None, op0=ALU.mult), r=pT.r(0) + [self.rcstb], w=pT.r(0))
                pso, rso = self.bank()
                psd, rsd = self.bank()
                nb = len(blocks)
                for n, (kap, vap, nk, which) in enumerate(blocks):
                    self.op("pe", lambda e, vap=vap, nk=nk, which=which, n=n: e.matmul(pso[:, 0:N], lhsT=vap, rhs=pT.ap[0:nk, which, 0:N],
                                                                                     start=(n == 0), stop=(n == nb - 1)),
                            r=rd + pT.r(which), w=[rso])
                for n, (kap, vap, nk, which) in enumerate(blocks):
                    self.op("pe", lambda e, nk=nk, which=which, n=n: e.matmul(psd[:, 0:N], lhsT=self.onesb[0:nk, :], rhs=pT.ap[0:nk, which, 0:N],
                                                                            start=(n == 0), stop=(n == nb - 1)),
                            r=[self.rcstb] + pT.r(which), w=[rsd])
                for hh in range(4):
                    head = 8 * g + 2 * hh + par
                    self.op("act", lambda e, hh=hh, head=head: e.activation(out=rec.ap[b:b + 64, 0, hh * nq:(hh + 1) * nq],
                                                                             in_=psd[b:b + 64, hh * nq:(hh + 1) * nq], func=AF.Ln,
                                                                             bias=self.esk[b:b + 64, head:head + 1]),
                            r=[rsd, self.r_esk], w=rec.all())
                self.op("act", lambda e: e.activation(out=rec.ap[b:b + 64, 0, 0:N], in_=rec.ap[b:b + 64, 0, 0:N], func=AF.Exp, scale=-1.0),
                        r=rec.all(), w=rec.all())
                self.op("dve", lambda e, g=g: e.tensor_tensor(out=outf(g, b), in0=pso[b:b + 64, 0:N].rearrange("p (h q) -> p h q", h=4),
                                                              in1=rec.ap[b:b + 64, 0, 0:N].rearrange("p (h q) -> p h q", h=4), op=ALU.mult),
                        r=[rso] + rec.all(), w=wr(g))


            for g in range(4):
                for par in range(2):
                    one(g, par)

        if mode == "p":
            for qb in range(4):
                qs = slice(qb * 128, (qb + 1) * 128)
                if qb > 0:
                    ps_ = slice((qb - 1) * 128, qb * 128)
                    kp = lambda g, b, ps_=ps_: kT.ap[b:b + 64, g, ps_]
                    vp = lambda g, qb=qb: vd.ap[:, qb - 1, g * 128:(g + 1) * 128]
                elif ti > 0:
                    kp = lambda g, b: self.kcar[b:b + 64, g, :]
                    vp = lambda g: self.vcar[:, g * 128:(g + 1) * 128]
                else:
                    kp = vp = None
                attend(kp, vp,
                       lambda g, b, qs=qs: kT.ap[b:b + 64, g, qs],
                       lambda g, qb=qb: vd.ap[:, qb, g * 128:(g + 1) * 128],
                       lambda g, b, qs=qs: qT.ap[b:b + 64, 4 * g:4 * g + 4, qs],
                       lambda g, b, qs=qs: oT.ap[b:b + 64, 4 * g:4 * g + 4, qs],
                       128, 128, False,
                       qT.all() + kT.all() + vd.all() + [self.r_kcar, self.r_vcar],
                       lambda g: oT.r(4 * g, 4 * g + 4))
            self.op("act", lambda e: e.copy(out=self.kcar[:, :, :], in_=kT.ap[:, :, TT - 128:TT]), r=kT.all(), w=[self.r_kcar])
            self.op("act", lambda e: e.copy(out=self.vcar[:, :], in_=vd.ap[:, 3, :]), r=vd.all(), w=[self.r_vcar])
        else:
            for s in range(NS):
                self.dma(lambda e, s=s: e.dma_start(out=aux.ap[:, 0, :], in_=self.i_skT[s]), w=aux.r(0))
                self.dma(lambda e, s=s: e.dma_start(out=aux.ap[:, 1, :], in_=self.i_svd[s]), w=aux.r(1))
                self.op("act", lambda e: e.copy(out=self.kcar[:, :, :], in_=aux.ap[:, 0, :].rearrange("p (g k) -> p g k", g=4)),
                        r=aux.r(0), w=[self.r_kcar])
                self.op("dve", lambda e: e.tensor_copy(out=self.vcar[:, :], in_=aux.ap[:, 1, :]), r=aux.r(1), w=[self.r_vcar])
                attend(lambda g, b: self.kcar[b:b + 64, g, :],
                       lambda g: self.vcar[:, g * 128:(g + 1) * 128],
                       lambda g, b, s=s: kT.ap[b:b + 64, g, s:s + 1],
                       lambda g, s=s: vd.ap[0:1, s, g * 128:(g + 1) * 128],
                       lambda g, b, s=s: qT.ap[b:b + 64, 4 * g:4 * g + 4, s:s + 1],
                       lambda g, b, s=s: oT.ap[b:b + 64, 4 * g:4 * g + 4, s:s + 1],
                       1, 1, True,
                       qT.all() + kT.all() + vd.all() + [self.r_kcar, self.r_vcar],
                       lambda g: oT.r(4 * g, 4 * g + 4))
                self.dma(lambda e, s=s: e.dma_start(out=self.o_skc[s], in_=self.i_skraw[s, 1:128, :]))
                self.dma(lambda e, s=s: e.dma_start(out=self.o_svc[s], in_=self.i_svraw[s, 1:128, :]))
        self.proj_fm(self.w_swa_o, NDC, oT.ap, lambda kc: oT.r(kc), C, 0, NDC,
                     lambda j, pst, prg: self.resid_add(j, pst, prg, C, bias=self.P("swa_bo", j)))

    def hgrn(self, layer, mode, C, last=False):
        W_in, W_o = self.w_hg_in, self.w_hg_o
        self.rmsnorm("nmix%d" % layer, C)
        vtm = Buf(self, 0, 4, 2048, BF16, "vtm")
        gs = Buf(self, vtm.end, 16, TT, BF16, "gs")
        ebk = Buf(self, gs.end, 4, TT, F32, "ebk")
        kt = Buf(self, ebk.end, 4, TT, BF16, "kt")
        qt = Buf(self, kt.end, 4, TT, BF16, "qt")
        khm = Buf(self, qt.end, 4, 512, BF16, "khm")
        T = Buf(self, khm.end, 6, TT, F32, "hgT")
        at = Buf(self, T.end, 8, 128, BF16, "at")
        Sbf = Buf(self, at.end, 16, 128, BF16, "Sbf")
        lb = lambda c: self.hlb[:, 0, c:c + 1]
        omlb = lambda c: self.hlb[:, 1, c:c + 1]

        if mode == "p":
            for c in range(16):
                self.op("act", lambda e, c=c: e.copy(out=Sbf.ap[:, c, :], in_=self.Sst[:, c, :]), r=[self.rS[c]], w=Sbf.r(c))
            def v_group(cg):
                slab, rs = self.load_slab(W_in, 0, NDC, 4096 + cg * 512, 512)
                for blk in range(4):
                    pst, prg = self.bank()
                    for kc in range(NDC):
                        self.op("pe", lambda e, kc=kc, blk=blk, pst=pst, slab=slab: e.matmul(
                            pst[:, :], lhsT=self.xn[:, kc, blk * 128:(blk + 1) * 128], rhs=slab[:, kc, :], start=(kc == 0), stop=(kc == NDC - 1)),
                            r=rs(kc) + [self.rxn[kc]], w=[prg])
                    self.op("act", lambda e, blk=blk, cg=cg, pst=pst: e.copy(out=vtm.ap[:, blk, cg * 512:(cg + 1) * 512], in_=pst[:, :]),
                            r=[prg], w=vtm.r(blk))
        else:
            vts = self.arena[:, T.off:T.off + 2048]
            rvts = self.aregs(T.off, T.off + 2048)
            for cg in range(4):
                slab, rs = self.load_slab(W_in, 0, NDC, 4096 + cg * 512, 512)
                pst, prg = self.bank()
                for kc in range(NDC):
                    self.op("pe", lambda e, kc=kc, pst=pst, slab=slab: e.matmul(pst[0:NS, :], lhsT=self.xn[:, kc, 0:NS], rhs=slab[:, kc, :],
                                                                               start=(kc == 0), stop=(kc == NDC - 1)), r=rs(kc) + [self.rxn[kc]], w=[prg])
                self.op("act", lambda e, cg=cg, pst=pst: e.copy(out=vts[0:NS, cg * 512:(cg + 1) * 512], in_=pst[0:NS, :]), r=[prg], w=rvts)

        def evac_g(j, pst, prg):
            self.op("act", lambda e: e.activation(out=gs.ap[:, j, 0:C], in_=pst[:, 0:C], func=AF.Silu), r=[prg], w=gs.r(j))

        if mode != "p":
            self.proj_fm(W_in, NDC, self.xn, self.rxn_f, C, 6144, 16, evac_g)

        if mode == "p":
            f_, lf, bb, enb, kh, qf = (T.ap[:, k, 0:C] for k in range(6))
            rf, rlf, rbb, renb, rkh, rqf = (T.r(k) for k in range(6))
            for hg in range(4):
                zb = []

                def evac_z(j, pst, prg, hg=hg, zb=zb):
                    zb.append((j, pst, prg))
                    self._reserved.add(self.ps.index(pst))

                self.proj_fm(W_in, NDC, self.xn, self.rxn_f, C, 2048 + hg * 512, 4, evac_z)
                held = set(self.ps.index(z[1]) for z in zb)
                self.proj_fm(W_in, NDC, self.xn, self.rxn_f, C, 6144 + hg * 512, 4, lambda j, pst, prg, hg=hg: evac_g(hg * 4 + j, pst, prg))
                self._reserved.update(held)
                v_group(hg)
                self.hg_zstage(hg, C, zb, ebk, kt, khm, T)

                def evac_q(j, pst, prg, hg=hg):
                    self.op("act", lambda e: e.activation(out=qf, in_=pst[:, 0:C], func=AF.Silu), r=[prg], w=rqf)
                    self.op("dve", lambda e: e.tensor_tensor(out=qt.ap[:, j, 0:C], in0=qf, in1=ebk.ap[:, j, 0:C], op=ALU.mult),
                            r=rqf + ebk.r(j), w=qt.r(j))

                self.proj_fm(W_in, NDC, self.xn, self.rxn_f, C, hg * 512, 4, evac_q)
                self.hg_chunks(hg, C, last, vtm, gs, ebk, kt, qt, khm, T, at, Sbf)
        else:
            fz = self.arena[:, ebk.off:ebk.off + 64].rearrange("p (c s) -> p c s", c=16)
            kz = self.arena[:, ebk.off + 64:ebk.off + 128].rearrange("p (c s) -> p c s", c=16)
            qz = self.arena[:, ebk.off + 128:ebk.off + 192].rearrange("p (c s) -> p c s", c=16)
            oz = self.arena[:, ebk.off + 192:ebk.off + 256].rearrange("p (c s) -> p c s", c=16)
            rz = self.aregs(ebk.off, ebk.off + 256)

            def evac_zs(c, pst, prg):
                self.op("act", lambda e: e.activation(out=fz[:, c, :], in_=pst[:, 0:C], func=AF.Sigmoid), r=[prg], w=rz)
                self.op("dve", lambda e: e.tensor_scalar(out=fz[:, c, :], in0=fz[:, c, :], scalar1=omlb(c), scalar2=lb(c), op0=ALU.mult, op1=ALU.add),
                        r=rz + [self.r_hlb], w=rz)
                self.op("dve", lambda e: e.tensor_scalar(out=kz[:, c, :], in0=fz[:, c, :], scalar1=-1.0, scalar2=1.0, op0=ALU.mult, op1=ALU.add),
                        r=rz, w=rz)

            self.proj_fm(W_in, NDC, self.xn, self.rxn_f, C, 2048, 16, evac_zs)

            def evac_qs(c, pst, prg):
                self.op("act", lambda e: e.activation(out=qz[:, c, :], in_=pst[:, 0:C], func=AF.Silu), r=[prg], w=rz)

            self.proj_fm(W_in, NDC, self.xn, self.rxn_f, C, 0, 16, evac_qs)
            Sl = Buf(self, kt.off, 4, 128, F32, "Sl")
            tq = Buf(self, Sl.end, 2, 128, F32, "tq")
            n = 0
            for s in range(NS):
                for c in range(16):
                    k = n % 2
                    n += 1
                    self.dma(lambda e, s=s, c=c, k=k: e.dma_start(out=Sl.ap[:, k, :], in_=self.i_sS[s, c]), w=Sl.r(k))
                    pv, rpv = self.bank()
                    self.op("pe", lambda e, s=s, c=c, pv=pv: e.matmul(pv[:, 0:128], lhsT=self.sel[0:NS, s * 128:(s + 1) * 128],
                                                                     rhs=vts[0:NS, c * 128:(c + 1) * 128], start=True, stop=True),
                            r=[self.rcst] + rvts, w=[rpv])
                    self.op("dve", lambda e, s=s, c=c, k=k, pv=pv: e.tensor_scalar(out=tq.ap[:, k, :], in0=pv[:, 0:128], scalar1=kz[:, c, s:s + 1],
                                                                                  scalar2=None, op0=ALU.mult), r=[rpv] + rz, w=tq.r(k))
                    self.op("dve", lambda e, s=s, c=c, k=k: e.scalar_tensor_tensor(out=Sl.ap[:, 2 + k, :], in0=Sl.ap[:, k, :], scalar=fz[:, c, s:s + 1],
                                                                                  in1=tq.ap[:, k, :], op0=ALU.mult, op1=ALU.add),
                            r=Sl.r(k) + tq.r(k) + rz, w=Sl.r(2 + k))
                    self.dma(lambda e, s=s, c=c, k=k: e.dma_start(out=self.o_sS[s, c], in_=Sl.ap[:, 2 + k, :]), r=Sl.r(2 + k))
                    po, rpo = self.bank()
                    self.op("pe", lambda e, s=s, c=c, k=k, po=po: e.matmul(po[:, 0:1], lhsT=Sl.ap[:, 2 + k, :], rhs=qz[:, c, s:s + 1],
                                                                          start=True, stop=True), r=Sl.r(2 + k) + rz, w=[rpo])
                    self.op("act", lambda e, s=s, c=c, po=po: e.copy(out=oz[:, c, s:s + 1], in_=po[:, 0:1]), r=[rpo], w=rz)
            ozf = self.arena[:, ebk.off + 192:ebk.off + 256]
            osq = T.ap[:, 4, 0:64]
            rsn = T.ap[:, 5, 0:64]
            pn, rpn = self.bank()
            self.op("act", lambda e: e.activation(out=osq, in_=ozf, func=AF.Square), r=rz, w=T.r(4))
            self.op("pe", lambda e: e.matmul(pn[:, 0:64], lhsT=self.ones, rhs=osq, start=True, stop=True), r=T.r(4) + [self.rcst], w=[rpn])
            self.op("act", lambda e: e.activation(out=rsn, in_=pn[:, 0:64], func=AF.Sqrt, scale=1.0 / 128, bias=self.eps_t[:]),
                    r=[rpn, self.r_eps], w=T.r(5))
            self.op("dve", lambda e: e.reciprocal(out=rsn, in_=rsn), r=T.r(5), w=T.r(5))
            self.op("dve", lambda e: e.scalar_tensor_tensor(out=osq, in0=ozf, scalar=self.P("hg_norm", 0), in1=rsn, op0=ALU.mult, op1=ALU.mult),
                    r=rz + T.r(5) + [self.rpp], w=T.r(4))
            self.op("dve", lambda e: e.tensor_tensor(out=gs.ap[:, :, 0:NS], in0=osq.rearrange("p (c s) -> p c s", c=16), in1=gs.ap[:, :, 0:NS],
                                                     op=ALU.mult), r=T.r(4) + gs.all(), w=gs.all())
        self.proj_fm(W_o, NDC, gs.ap, lambda kc: gs.r(kc), C, 0, NDC, lambda j, pst, prg: self.resid_add(j, pst, prg, C))

    def hg_zstage(self, hg, C, zb, ebk, kt, khm, T):
        lb = lambda c: self.hlb[:, 0, c:c + 1]
        omlb = lambda c: self.hlb[:, 1, c:c + 1]
        for p0 in (0, 2):
            hs = []
            for k in range(2):
                j, pst, prg = zb[p0 + k]
                hs.append((j, hg * 4 + j, pst, prg, [T.ap[:, 3 * k + q, 0:C] for q in range(3)], [T.r(3 * k + q) for q in range(3)]))
            for (j, c, pst, prg, (A, B, Cc), (rA, rB, rC)) in hs:
                self.op("act", lambda e, A=A, pst=pst: e.activation(out=A, in_=pst[:, 0:C], func=AF.Sigmoid), r=[prg], w=rA)
                self._reserved.discard(self.ps.index(pst))
            for (j, c, pst, prg, (A, B, Cc), (rA, rB, rC)) in hs:
                self.op("dve", lambda e, A=A, c=c: e.tensor_scalar(out=A, in0=A, scalar1=omlb(c), scalar2=lb(c), op0=ALU.mult, op1=ALU.add),
                        r=rA + [self.r_hlb], w=rA)
            for (j, c, pst, prg, (A, B, Cc), (rA, rB, rC)) in hs:
                self.op("act", lambda e, A=A, B=B: e.activation(out=B, in_=A, func=AF.Ln), r=rA, w=rB)
            for (j, c, pst, prg, (A, B, Cc), (rA, rB, rC)) in hs:
                self.op("dve", lambda e, B=B, Cc=Cc: e.tensor_tensor_scan(out=Cc, data0=self.m01[:, 0:C], data1=B, initial=0.0, op0=ALU.mult, op1=ALU.add),
                        r=rB + [self.rcst], w=rC)
            for (j, c, pst, prg, (A, B, Cc), (rA, rB, rC)) in hs:
                self.op("act", lambda e, Cc=Cc, j=j: e.activation(out=ebk.ap[:, j, 0:C], in_=Cc, func=AF.Exp), r=rC, w=ebk.r(j))
                self.op("act", lambda e, Cc=Cc, B=B: e.activation(out=B, in_=Cc, func=AF.Exp, scale=-1.0), r=rC, w=rB)
            for (j, c, pst, prg, (A, B, Cc), (rA, rB, rC)) in hs:
                self.op("dve", lambda e, A=A: e.tensor_scalar(out=A, in0=A, scalar1=-1.0, scalar2=1.0, op0=ALU.mult, op1=ALU.add), r=rA, w=rA)
                self.op("dve", lambda e, A=A, B=B: e.tensor_tensor(out=A, in0=A, in1=B, op=ALU.mult), r=rA + rB, w=rA)
            for (j, c, pst, prg, (A, B, Cc), (rA, rB, rC)) in hs:
                self.op("act", lambda e, A=A, j=j: e.copy(out=kt.ap[:, j, 0:C], in_=A), r=rA, w=kt.r(j))
            for (j, c, pst, prg, (A, B, Cc), (rA, rB, rC)) in hs:
                for ch in range(C // 64):
                    cs = slice(ch * 64, (ch + 1) * 64)
                    lastc = ch * 64 + 63
                    self.op("dve", lambda e, cs=cs, lastc=lastc, A=A, Cc=Cc, j=j: e.tensor_scalar(
                        out=Cc[:, cs], in0=A[:, cs], scalar1=ebk.ap[:, j, lastc:lastc + 1], scalar2=None, op0=ALU.mult),
                        r=rA + ebk.r(j), w=rC)
            for blk in range(4):
                for (j, c, pst, prg, (A, B, Cc), (rA, rB, rC)) in hs:
                    pt_, rpt = self.bank()
                    self.op("pe", lambda e, blk=blk, pt_=pt_, Cc=Cc: e.transpose(pt_[:, 0:128], Cc[:, blk * 128:(blk + 1) * 128], self.ident),
                            r=rC + [self.rcst], w=[rpt])
                    self.op("act", lambda e, blk=blk, pt_=pt_, j=j: e.copy(out=khm.ap[:, j, blk * 128:(blk + 1) * 128], in_=pt_[:, 0:128]),
                            r=[rpt], w=khm.r(j))

    def hg_chunks(self, hg, C, last, vtm, gs, ebk, kt, qt, khm, T, at, Sbf):
        osb = [(T.ap[:, k, 0:C], T.r(k)) for k in range(4)]
        for blk in range(4):
            cs = slice(blk * 128, (blk + 1) * 128)
            pos = []
            for j in range(4):
                pos.append(self.bank())
                self._reserved.add(self.ps.index(pos[j][0]))
            for j in range(4):
                c = hg * 4 + j
                po, rpo = pos[j]
                pa, rpa = self.bank()
                self.op("pe", lambda e, cs=cs, pa=pa, j=j: e.matmul(pa[:, 0:128], lhsT=kt.ap[:, j, cs], rhs=qt.ap[:, j, cs], start=True, stop=True),
                        r=kt.r(j) + qt.r(j), w=[rpa])
                a_ap, a_r = at.ap[:, 2 * j + blk % 2, :], at.r(2 * j + blk % 2)
                self.op("dve", lambda e, pa=pa, a_ap=a_ap: e.tensor_tensor(out=a_ap, in0=pa[:, 0:128], in1=self.mhg, op=ALU.mult),
                        r=[rpa, self.rcstb], w=a_r)
                self.op("pe", lambda e, blk=blk, po=po, a_ap=a_ap, c=c: e.matmul(po[:, 0:128], lhsT=vtm.ap[:, blk, c * 128:(c + 1) * 128], rhs=a_ap,
                                                                                start=True, stop=False), r=vtm.r(blk) + a_r, w=[rpo])
            for half in range(2):
                hs = slice(blk * 128 + half * 64, blk * 128 + half * 64 + 64)
                lastc = blk * 128 + half * 64 + 63
                pSs = []
                for j in range(4):
                    c = hg * 4 + j
                    po, rpo = pos[j]
                    self.op("pe", lambda e, half=half, hs=hs, po=po, c=c, j=j: e.matmul(po[:, half * 64:(half + 1) * 64], lhsT=Sbf.ap[:, c, :],
                                                                                       rhs=qt.ap[:, j, hs], start=False, stop=(half == 1)),
                            r=Sbf.r(c) + qt.r(j), w=[rpo])
                    pS, rpS = self.bank()
                    pSs.append((pS, rpS))
                    self.op("pe", lambda e, half=half, blk=blk, pS=pS, c=c, j=j: e.matmul(
                        pS[:, 0:128], lhsT=khm.ap[half * 64:(half + 1) * 64, j, blk * 128:(blk + 1) * 128],
                        rhs=vtm.ap[half * 64:(half + 1) * 64, blk, c * 128:(c + 1) * 128], start=True, stop=True),
                        r=khm.r(j) + vtm.r(blk), w=[rpS])
                for j in range(4):
                    c = hg * 4 + j
                    pS, rpS = pSs[j]
                    self.op("dve", lambda e, pS=pS, lastc=lastc, c=c, j=j: e.scalar_tensor_tensor(
                        out=self.Sst[:, c, :], in0=self.Sst[:, c, :], scalar=ebk.ap[:, j, lastc:lastc + 1], in1=pS[:, 0:128],
                        op0=ALU.mult, op1=ALU.add), r=[self.rS[c], rpS] + ebk.r(j), w=[self.rS[c]])
                for j in range(4):
                    c = hg * 4 + j
                    self.op("act", lambda e, c=c: e.copy(out=Sbf.ap[:, c, :], in_=self.Sst[:, c, :]), r=[self.rS[c]], w=Sbf.r(c))
            for j in range(4):
                po, rpo = pos[j]
                self._reserved.discard(self.ps.index(po))
                self.op("act", lambda e, cs=cs, po=po, j=j: e.copy(out=osb[j][0][:, cs], in_=po[:, 0:128]), r=[rpo], w=osb[j][1])
        for j in range(4):
            c = hg * 4 + j
            self.head_norm(osb[j][0], osb[j][1], T.ap[:, 5, 0:C], T.r(5), gs, c, C, T.ap[:, 4, 0:C], T.r(4))
            if last:
                self.dma(lambda e, c=c: e.dma_start(out=self.o_pS[c], in_=self.Sst[:, c, :]), r=[self.rS[c]])

    def head_norm(self, osb, rosb, osq, rosq, gs, c, C, tmp, rtmp):
        pn, rpn = self.bank()
        self.op("act", lambda e: e.activation(out=osq, in_=osb, func=AF.Square), r=rosb, w=rosq)
        self.op("pe", lambda e: e.matmul(pn[:, 0:C], lhsT=self.ones, rhs=osq, start=True, stop=True), r=rosq + [self.rcst], w=[rpn])
        self.op("act", lambda e: e.activation(out=tmp, in_=pn[:, 0:C], func=AF.Sqrt, scale=1.0 / 128, bias=self.eps_t[:]),
                r=[rpn, self.r_eps], w=rtmp)
        self.op("dve", lambda e: e.reciprocal(out=tmp, in_=tmp), r=rtmp, w=rtmp)
        self.op("dve", lambda e: e.scalar_tensor_tensor(out=osq, in0=osb, scalar=self.P("hg_norm", 0), in1=tmp, op0=ALU.mult, op1=ALU.mult),
                r=rosb + rtmp + [self.rpp], w=rosq)
        self.op("dve", lambda e: e.tensor_tensor(out=gs.ap[:, c, 0:C], in0=osq, in1=gs.ap[:, c, 0:C], op=ALU.mult), r=rosq + gs.r(c), w=gs.r(c))

    def build(self, pp_off, npp, nlayers=DEPTH, do_prompt=True, do_sample=True, ntiles=NT):
        self.setup(pp_off, npp)
        self.xT = self.inp("xT", [D, SEQ])
        self.xsT = self.inp("xsT", [D, NS])
        self.w_lru_in = [self.inp("lru_w_in%d" % i, [D, 2 * DR]) for i in range(2)]
        self.w_lru_out = [self.inp("lru_w_out%d" % i, [DR, D]) for i in range(2)]
        self.w_lru_a = [self.inp("lru_wa%d" % i, [128, 64 * 128]) for i in range(2)]
        self.w_lru_i = [self.inp("lru_wi%d" % i, [128, 64 * 128]) for i in range(2)]
        self.w_qk = self.inp("swa_wqk", [D, 2560])
        self.w_qkr = self.inp("swa_wqkr", [D, 2560])
        self.w_vd = self.inp("swa_wvd", [D, 512])
        self.w_swa_o = self.inp("swa_wo", [D, D])
        self.w_hg_in = self.inp("hg_w_in", [D, 8192])
        self.w_hg_o = self.inp("hg_w_o", [D, D])
        self.w_ffn_in = [self.inp("ffn_w_in%d" % i, [D, 2 * DFF]) for i in range(DEPTH)]
        self.w_ffn_out = [self.inp("ffn_w_out%d" % i, [DFF, D]) for i in range(DEPTH)]
        self.rope_d = self.inp("rope", [128, 2, SEQ + NS])
        self.bvb_d = self.inp("bvb", [128, 512])
        self.i_sh = self.inp("s_h", [128, 2 * NRC * NS])
        self.i_sc = self.inp("s_c", [128, 2 * NRC * 3 * NS])
        self.i_sf = self.inp("s_f", [128, DEPTH * NFC * 2 * NS])
        self.i_skT = self.inp("s_kT", [NS, 128, 512])
        self.i_svd = self.inp("s_vd", [NS, 128, 512])
        self.i_skraw = self.inp("s_kraw", [NS, 128, 256])
        self.i_svraw = self.inp("s_vraw", [NS, 128, 256])
        self.i_sS = self.inp("s_S", [NS, 16, 128, 128])
        self.o_yT = self.outp("yT", [D, SEQ])
        self.o_ysT = self.outp("ysT", [D, NS])
        self.o_plh = self.outp("p_lru_h", [128, 2 * NRC])
        self.o_plc = self.outp("p_lru_c", [128, 2 * NRC * 3])
        self.o_pk = self.outp("p_k", [128, 4, 128])
        self.o_pv = self.outp("p_v", [128, 512])
        self.o_pS = self.outp("p_S", [16, 128, 128])
        self.o_pf = self.outp("p_f", [128, DEPTH * NFC * 2])
        self.o_slh = self.outp("o_s_h", [128, 2 * NRC * NS])
        self.o_slc = self.outp("o_s_c", [128, 2 * NRC * 3 * NS])
        self.o_sf = self.outp("o_s_f", [128, DEPTH * NFC * 2 * NS])
        self.o_skc = self.outp("o_s_kc", [NS, 127, 256])
        self.o_svc = self.outp("o_s_vc", [NS, 127, 256])
        self.o_sknew = self.outp("o_s_knew", [128, 4, NS])
        self.o_svnew = self.outp("o_s_vnew", [NS, 512])
        self.o_sS = self.outp("o_s_S", [NS, 16, 128, 128])
        self.precompute()

        def layers(mode, C, ti, last):
            for layer in range(nlayers):
                kind = MIX[layer]
                if getattr(self, "skip_mix", False):
                    pass
                elif kind == 0:
                    self.lru(layer, mode, C)
                elif kind == 1:
                    self.swa(layer, mode, C, ti=ti, last=last)
                else:
                    self.hgrn(layer, mode, C, last=last)
                if not getattr(self, "skip_ffn", False):
                    self.ffn(layer, mode, C)
            yb = Buf(self, 0, NDC, TT, F32, "ybuf")
            self.rmsnorm("nfinal", C, out=lambda c: yb.ap[:, c, 0:C], rout=lambda c: yb.r(c))
            return yb

        if do_prompt:
            for ti in range(ntiles):
                last = (ti == ntiles - 1)
                for c in range(NDC):
                    self.dma(lambda e, c=c, ti=ti: e.dma_start(out=self.x[:, c, :], in_=self.xT[c * 128:(c + 1) * 128, ti * TT:(ti + 1) * TT]),
                             w=[self.rx[c]])
                yb = layers("p", TT, ti, last)
                for c in range(NDC):
                    self.dma(lambda e, c=c, ti=ti, yb=yb: e.dma_start(out=self.o_yT[c * 128:(c + 1) * 128, ti * TT:(ti + 1) * TT], in_=yb.ap[:, c, :]),
                             r=yb.r(c))
            self.dma(lambda e: e.dma_start(out=self.o_plh, in_=self.lru_h[:].rearrange("p a b -> p (a b)")), r=self.r_lh)
            self.dma(lambda e: e.dma_start(out=self.o_plc, in_=self.lru_cb[:].rearrange("p a b c -> p (a b c)")), r=self.r_lcb)
            self.dma(lambda e: e.dma_start(out=self.o_pf, in_=self.ffn_cb[:].rearrange("p a b c -> p (a b c)")), r=self.r_fcb)
        if do_sample:
            self.dma(lambda e: e.dma_start(out=self.s_h[:].rearrange("p a b c -> p (a b c)"), in_=self.i_sh), w=self.r_sh)
            self.dma(lambda e: e.dma_start(out=self.s_c[:].rearrange("p a b c d -> p (a b c d)"), in_=self.i_sc), w=self.r_sc)
            self.dma(lambda e: e.dma_start(out=self.s_f[:].rearrange("p a b c d -> p (a b c d)"), in_=self.i_sf), w=self.r_sf)
            for c in range(NDC):
                self.dma(lambda e, c=c: e.dma_start(out=self.x[:, c, 0:NS], in_=self.xsT[c * 128:(c + 1) * 128, :]), w=[self.rx[c]])
            yb = layers("s", NS, 0, False)
            for c in range(NDC):
                self.dma(lambda e, c=c, yb=yb: e.dma_start(out=self.o_ysT[c * 128:(c + 1) * 128, :], in_=yb.ap[:, c, 0:NS]), r=yb.r(c))
            self.dma(lambda e: e.dma_start(out=self.o_slh, in_=self.s_h[:].rearrange("p a b c -> p (a b c)")), r=self.r_sh)
            self.dma(lambda e: e.dma_start(out=self.o_slc, in_=self.s_c[:].rearrange("p a b c d -> p (a b c d)")), r=self.r_sc)
            self.dma(lambda e: e.dma_start(out=self.o_sf, in_=self.s_f[:].rearrange("p a b c d -> p (a b c d)")), r=self.r_sf)
        self.S.emit(self.st)
        self.st.close()
        return self.nc
def make_shared(inp):
    sh = {}
    pp = PP()
    for l in range(DEPTH):
        pp.add("nmix%d" % l, fm(inp["norm_mix"][l]))
        pp.add("nffn%d" % l, fm(inp["norm_ffn"][l]))
    pp.add("nfinal", fm(inp["norm_final"]))
    for j in range(2):
        pp.add("lru%d_cw" % j, fm(inp["lru_conv_w"][j]))
        pp.add("lru%d_cb" % j, fm(inp["lru_conv_b"][j]))
        pp.add("lru%d_ba" % j, fm(inp["lru_b_a"][j]))
        pp.add("lru%d_bi" % j, fm(inp["lru_b_i"][j]))
        pp.add("lru%d_lam" % j, fm(inp["lru_lambda"][j]))
        sh["lru_w_in%d" % j] = np.ascontiguousarray(inp["lru_w_in"][j])
        sh["lru_w_out%d" % j] = np.ascontiguousarray(inp["lru_w_out"][j])
        sh["lru_wa%d" % j] = band_layout(inp["lru_w_a"][j])
        sh["lru_wi%d" % j] = band_layout(inp["lru_w_i"][j])
    cols, rot = qk_cols()
    vc = v_cols()
    wqkv = inp["swa_w_qkv"][0]
    bqkv = inp["swa_b_qkv"][0]
    sh["swa_wqk"] = np.ascontiguousarray(wqkv[:, cols])
    sh["swa_wqkr"] = np.ascontiguousarray(wqkv[:, rot])
    sh["swa_wvd"] = np.ascontiguousarray(wqkv[:, vc])
    sh["swa_wo"] = np.ascontiguousarray(inp["swa_w_o"][0])
    pp.add("swa_bqk", fm(bqkv[cols]))
    pp.add("swa_bqkr", fm(bqkv[rot]))
    pp.add("swa_bo", fm(inp["swa_b_o"][0]))
    pp.add("swa_sinks", np.tile(np.asarray(inp["swa_sinks"][0], np.float32)[None, :], (128, 1)))
    sh["bvb"] = np.ascontiguousarray(np.tile(np.asarray(bqkv[vc], np.float32)[None, :], (128, 1)))
    sh["hg_w_in"] = np.ascontiguousarray(inp["hg_w_in"][0])
    sh["hg_w_o"] = np.ascontiguousarray(inp["hg_w_o"][0])
    pp.add("hg_lbl", fm(inp["hg_lb_logits"]))
    pp.add("hg_norm", fm(inp["hg_norm"][0]))
    for l in range(DEPTH):
        pp.add("ffn%d_cw" % l, fm(inp["ffn_conv_w"][l]))
        pp.add("ffn%d_cb" % l, fm(inp["ffn_conv_b"][l]))
        sh["ffn_w_in%d" % l] = np.ascontiguousarray(inp["ffn_w_in"][l])
        sh["ffn_w_out%d" % l] = np.ascontiguousarray(inp["ffn_w_out"][l])
    sh["pp"] = pp.array()
    cst = np.zeros((128, NCST), np.float32)
    cst[:, 0:128] = np.eye(128, dtype=np.float32)
    cst[:, 128:256] = 1.0
    m01 = np.ones((512,), np.float32)
    m01[::64] = 0.0
    cst[:, 256:768] = m01[None, :]
    for s in range(NS):
        cst[s, 768 + s * 128:768 + (s + 1) * 128] = 1.0
    sh["cst"] = cst
    j = np.arange(128)[:, None]
    i = np.arange(128)[None, :]
    msk = np.zeros((128, NMSK), np.float32)
    msk[:, 0:512] = np.tile((j > i).astype(np.float32), (1, 4))
    msk[:, 512:1024] = np.tile((j <= i).astype(np.float32), (1, 4))
    msk[:, 1024:1152] = ((j // 64 == i // 64) & (j <= i)).astype(np.float32)
    sh["msk"] = msk
    Cp, Sp = rope_tables(np.arange(SEQ))
    Cs, Ss = rope_tables(np.full((NS,), PAST))
    rope = np.zeros((128, 2, SEQ + NS), np.float32)
    rope[:, 0, :SEQ] = Cp
    rope[:, 1, :SEQ] = Sp
    rope[:, 0, SEQ:] = Cs
    rope[:, 1, SEQ:] = Ss
    sh["rope"] = rope
    return sh, pp.off, pp.n


def make_core(inp, b, s0):
    d = {}
    d["xT"] = np.ascontiguousarray(np.asarray(inp["x_prompt"][b], np.float32).T)
    d["xsT"] = np.ascontiguousarray(np.asarray(inp["x_sample"][s0:s0 + NS, 0, :], np.float32).T)
    sl = slice(s0, s0 + NS)
    h = fm(inp["state_lru_h"][:, sl])
    d["s_h"] = np.ascontiguousarray(h.transpose(0, 1, 3, 2)).reshape(128, -1)
    c = fm(inp["state_lru_conv"][:, sl])
    d["s_c"] = np.ascontiguousarray(c.transpose(0, 1, 4, 3, 2)).reshape(128, -1)
    f = fm(inp["state_ffn_conv"][:, sl])
    d["s_f"] = np.ascontiguousarray(f.transpose(0, 1, 4, 3, 2)).reshape(128, -1)
    ck = np.asarray(inp["cache_swa_k"][0, sl], np.float32)
    cv = np.asarray(inp["cache_swa_v"][0, sl], np.float32)
    kT = ck.transpose(0, 3, 2, 1)
    d["s_kT"] = np.ascontiguousarray(np.concatenate([kT, kT], axis=1)).reshape(NS, 128, 512)
    d["s_vd"] = np.ascontiguousarray(np.concatenate([cv, cv], axis=3)).reshape(NS, 128, 512)
    d["s_kraw"] = np.ascontiguousarray(ck.reshape(NS, 128, 256))
    d["s_vraw"] = np.ascontiguousarray(cv.reshape(NS, 128, 256))
    d["s_S"] = np.ascontiguousarray(np.asarray(inp["state_hgrn"][0, sl], np.float32))
    return d


def unpack_core(r):
    o = {}
    o["y_p"] = r["yT"].T
    o["y_s"] = r["ysT"].T
    o["p_h"] = r["p_lru_h"].reshape(128, 2, NRC).transpose(1, 2, 0).reshape(2, DR)
    o["p_c"] = r["p_lru_c"].reshape(128, 2, NRC, 3).transpose(1, 3, 2, 0).reshape(2, 3, DR)
    o["p_k"] = r["p_k"][0:64].transpose(2, 1, 0)
    o["p_v"] = r["p_v"].reshape(128, 4, 128)[:, :, 0:64]
    o["p_S"] = r["p_S"]
    o["p_f"] = r["p_f"].reshape(128, DEPTH, NFC, 2).transpose(1, 3, 2, 0).reshape(DEPTH, 2, DFF)
    o["s_h"] = r["o_s_h"].reshape(128, 2, NRC, NS).transpose(1, 3, 2, 0).reshape(2, NS, DR)
    o["s_c"] = r["o_s_c"].reshape(128, 2, NRC, 3, NS).transpose(1, 4, 3, 2, 0).reshape(2, NS, 3, DR)
    knew = r["o_s_knew"][0:64].transpose(2, 1, 0)
    o["s_k"] = np.concatenate([r["o_s_kc"].reshape(NS, 127, 4, 64), knew[:, None]], axis=1)
    vnew = r["o_s_vnew"].reshape(NS, 4, 128)[:, :, 0:64]
    o["s_v"] = np.concatenate([r["o_s_vc"].reshape(NS, 127, 4, 64), vnew[:, None]], axis=1)
    o["s_S"] = r["o_s_S"]
    o["s_f"] = r["o_s_f"].reshape(128, DEPTH, NFC, 2, NS).transpose(1, 4, 3, 2, 0).reshape(DEPTH, NS, 2, DFF)
    return o


N_CORES = 8
PROMPT_CORE = (0, 2, 4, 6)


def kernel(**inputs):
    inp = {k: np.asarray(v) for k, v in inputs.items()}
    sh, pp_off, npp = make_shared(inp)
    nc = Builder().build(pp_off, npp)
    in_maps = []
    zero_xT = np.zeros((D, SEQ), np.float32)
    for i in range(N_CORES):
        m = dict(sh)
        m.update(make_core(inp, PROMPT_CORE.index(i) if i in PROMPT_CORE else 0, NS * i))
        if i not in PROMPT_CORE:
            m["xT"] = zero_xT
        in_maps.append(m)
    res = run_bass_kernel_spmd(nc, in_maps, core_ids=list(range(N_CORES)))
    outs = [unpack_core(r) for r in res.results]
    f32 = np.float32
    y_prompt = np.stack([outs[PROMPT_CORE[b]]["y_p"] for b in range(4)]).astype(f32)
    y_sample = np.concatenate([outs[i]["y_s"] for i in range(N_CORES)])[:, None, :].astype(f32)
    p_h = np.stack([outs[PROMPT_CORE[b]]["p_h"] for b in range(4)], axis=1).astype(f32)
    p_c = np.stack([outs[PROMPT_CORE[b]]["p_c"] for b in range(4)], axis=1).astype(f32)
    p_k = np.stack([outs[PROMPT_CORE[b]]["p_k"] for b in range(4)])[None].astype(f32)
    p_v = np.stack([outs[PROMPT_CORE[b]]["p_v"] for b in range(4)])[None].astype(f32)
    p_S = np.stack([outs[PROMPT_CORE[b]]["p_S"] for b in range(4)])[None].astype(f32)
    p_f = np.stack([outs[PROMPT_CORE[b]]["p_f"] for b in range(4)], axis=1).astype(f32)
    s_h = np.concatenate([outs[i]["s_h"] for i in range(N_CORES)], axis=1).astype(f32)
    s_c = np.concatenate([outs[i]["s_c"] for i in range(N_CORES)], axis=1).astype(f32)
    s_k = np.concatenate([outs[i]["s_k"] for i in range(N_CORES)])[None].astype(f32)
    s_v = np.concatenate([outs[i]["s_v"] for i in range(N_CORES)])[None].astype(f32)
    s_S = np.concatenate([outs[i]["s_S"] for i in range(N_CORES)])[None].astype(f32)
    s_f = np.concatenate([outs[i]["s_f"] for i in range(N_CORES)], axis=1).astype(f32)
    return tuple(np.ascontiguousarray(a) for a in
                 (y_prompt, y_sample, p_h, p_c, p_k, p_v, p_S, p_f, s_h, s_c, s_k, s_v, s_S, s_f))
```
